# Optimizing a Trainium2 kernel written in Bass

```python
import jax
import jax.numpy as jnp
from jax import lax
import numpy as np

D_MODEL = 2048
BATCH = 4
SEQ = 4096
DEPTH = 2

GROUP_WIDTH = D_MODEL // 4
GDN_HEAD_DIM = 128
GDN_HEADS = GROUP_WIDTH // GDN_HEAD_DIM
HGRN_HEAD_DIM = 128
HGRN_HEADS = GROUP_WIDTH // HGRN_HEAD_DIM
S5_CH = 16
S5_GROUPS = GROUP_WIDTH // S5_CH
S5_STATE = 64
LRU_BLOCKS = 8
LRU_BLOCK_DIM = GROUP_WIDTH // LRU_BLOCKS
LRU_C = 8.0
CONV_WIDTH = 4
CHUNK = 64
D_FF = ((8 * D_MODEL // 3 + 255) // 256) * 256
PLE_DIM = 256
DN_ALPHA = (2 * DEPTH) ** 0.25
DN_BETA = (8 * DEPTH) ** -0.25
LN_EPS = 1e-5
RMS_EPS = 1e-6
IN_SPLIT = (3 * GROUP_WIDTH, GROUP_WIDTH, GDN_HEADS, GDN_HEADS,
            GROUP_WIDTH, GROUP_WIDTH, GROUP_WIDTH, GROUP_WIDTH,
            GROUP_WIDTH,
            GROUP_WIDTH, GROUP_WIDTH)
IN_COLS = sum(IN_SPLIT)

kernel_name = 'hybrid_parallel_heads_gdn_hgrn2_s5_rglru'


def layer_norm(x, g, b):
    xf = x.astype(jnp.float32)
    mu = jnp.mean(xf, axis=-1, keepdims=True)
    xc = xf - mu
    var = jnp.mean(xc * xc, axis=-1, keepdims=True)
    y = xc * lax.rsqrt(var + LN_EPS) * g.astype(jnp.float32) + b.astype(jnp.float32)
    return y.astype(x.dtype)


def rms_norm(x, g):
    xf = x.astype(jnp.float32)
    y = xf * lax.rsqrt(jnp.mean(xf * xf, axis=-1, keepdims=True) + RMS_EPS) * g.astype(jnp.float32)
    return y.astype(x.dtype)


def l2norm(t):
    return t * lax.rsqrt(jnp.sum(t * t, axis=-1, keepdims=True) + RMS_EPS)


def swiglu(x, wi, wo):
    gate, up = jnp.split(x @ wi, 2, axis=-1)
    return (jax.nn.silu(gate) * up) @ wo


def causal_conv(x, w):
    k, c = w.shape
    return lax.conv_general_dilated(x, w[:, None, :].astype(x.dtype), window_strides=(1,),
                                    padding=((k - 1, 0),), dimension_numbers=('NWC', 'WIO', 'NWC'),
                                    feature_group_count=c)


def to_chunks(t):
    bsz, t_len, h, d = t.shape
    return t.reshape(bsz, t_len // CHUNK, CHUNK, h, d).transpose(1, 0, 3, 2, 4)


def from_chunks(t):
    nc, bsz, h, c, d = t.shape
    return t.transpose(1, 0, 3, 2, 4).reshape(bsz, nc * c, h, d)


def gated_deltanet(q, k, v, z, beta_logit, decay_logit, A_log, dt_bias, norm_g):
    f32 = jnp.float32
    dk = q.shape[-1]
    dv = v.shape[-1]
    q = l2norm(q.astype(f32)) * dk ** -0.5
    k = l2norm(k.astype(f32))
    v = v.astype(f32)
    beta = jax.nn.sigmoid(beta_logit.astype(f32))
    g = -jnp.exp(A_log.astype(f32)) * jax.nn.softplus(decay_logit.astype(f32) + dt_bias.astype(f32))
    qc, kc, vc = to_chunks(q), to_chunks(k), to_chunks(v)
    beta_c = to_chunks(beta[..., None])[..., 0]
    G = jnp.cumsum(to_chunks(g[..., None])[..., 0], axis=-1)
    idx = jnp.arange(CHUNK)
    incl = idx[:, None] >= idx[None, :]
    strict = idx[:, None] > idx[None, :]
    diff = G[..., :, None] - G[..., None, :]
    decay_incl = jnp.exp(jnp.where(incl, diff, -jnp.inf))
    decay_strict = jnp.where(strict, decay_incl, 0.0)
    a_mat = beta_c[..., :, None] * jnp.einsum('nbhtd,nbhsd->nbhts', kc, kc) * decay_strict
    rhs = jnp.concatenate([beta_c[..., None] * vc, (beta_c * jnp.exp(G))[..., None] * kc], axis=-1)
    sol = lax.linalg.triangular_solve(a_mat, rhs, left_side=True, lower=True, unit_diagonal=True)
    u, w = sol[..., :dv], sol[..., dv:]
    qk = jnp.einsum('nbhtd,nbhsd->nbhts', qc, kc) * decay_incl
    q_g = qc * jnp.exp(G)[..., None]
    k_d = kc * jnp.exp(G[..., -1:] - G)[..., None]
    g_last = jnp.exp(G[..., -1])

    def step(S, inp):
        qg_c, qk_c, u_c, w_c, kd_c, gl_c = inp
        u_hat = u_c - jnp.einsum('bhcd,bhde->bhce', w_c, S)
        o = jnp.einsum('bhcd,bhde->bhce', qg_c, S) + jnp.einsum('bhts,bhse->bhte', qk_c, u_hat)
        S = gl_c[..., None, None] * S + jnp.einsum('bhcd,bhce->bhde', kd_c, u_hat)
        return S, o

    S0 = jnp.zeros(qc.shape[1:3] + (dk, dv), f32)
    _, o = lax.scan(step, S0, (q_g, qk, u, w, k_d, g_last))
    o = rms_norm(from_chunks(o), norm_g) * jax.nn.silu(z.astype(f32))
    return o.reshape(o.shape[0], o.shape[1], -1)


def hgrn2(q, f_logit, i_in, g, lb, norm_g):
    f32 = jnp.float32
    dk = q.shape[-1]
    q = q.astype(f32) * dk ** -0.5
    lb = lb.astype(f32)
    log_f = jnp.logaddexp(jnp.log(lb), jnp.log1p(-lb) + jax.nn.log_sigmoid(f_logit.astype(f32)))
    k = -jnp.expm1(log_f)
    qc, kc, vc = to_chunks(q), to_chunks(k), to_chunks(i_in.astype(f32))
    bc = jnp.cumsum(to_chunks(log_f), axis=3)
    idx = jnp.arange(CHUNK)
    incl = (idx[:, None] >= idx[None, :])[:, :, None]

    def step(S, inp):
        q_c, k_c, v_c, b_c = inp
        diff = b_c[:, :, :, None, :] - b_c[:, :, None, :, :]
        decay = jnp.exp(jnp.where(incl, diff, -jnp.inf))
        att = jnp.einsum('bhtd,bhsd,bhtsd->bhts', q_c, k_c, decay)
        b_last = b_c[:, :, -1:, :]
        o = jnp.einsum('bhtd,bhde->bhte', q_c * jnp.exp(b_c), S) + jnp.einsum('bhts,bhse->bhte', att, v_c)
        S = jnp.exp(b_last[:, :, 0, :])[..., None] * S + jnp.einsum('bhsd,bhse->bhde', k_c * jnp.exp(b_last - b_c), v_c)
        return S, o

    S0 = jnp.zeros(qc.shape[1:3] + (dk, vc.shape[-1]), f32)
    _, o = lax.scan(step, S0, (qc, kc, vc, bc))
    o = rms_norm(from_chunks(o), norm_g) * jax.nn.silu(g.astype(f32))
    return o.reshape(o.shape[0], o.shape[1], -1)


def complex_combine(left, right):
    ar_l, ai_l, br_l, bi_l = left
    ar_r, ai_r, br_r, bi_r = right
    return (ar_r * ar_l - ai_r * ai_l,
            ar_r * ai_l + ai_r * ar_l,
            ar_r * br_l - ai_r * bi_l + br_r,
            ar_r * bi_l + ai_r * br_l + bi_r)


def real_combine(left, right):
    return (right[0] * left[0], right[0] * left[1] + right[1])


def s5(u, lam_re, lam_im, log_dt, B_re, B_im, C_re, C_im, D, glu_w, glu_b):
    f32 = jnp.float32
    bsz, t_len, width = u.shape
    uf = u.astype(f32).reshape(bsz, t_len, S5_GROUPS, S5_CH)
    lam_re = lam_re.astype(f32)
    lam_im = lam_im.astype(f32)
    dt = jnp.exp(log_dt.astype(f32))[:, None]
    mag = jnp.exp(lam_re * dt)
    ang = lam_im * dt
    ab_re, ab_im = mag * jnp.cos(ang), mag * jnp.sin(ang)
    den = lam_re * lam_re + lam_im * lam_im
    nr = ab_re - 1.0
    c_re = (nr * lam_re + ab_im * lam_im) / den
    c_im = (ab_im * lam_re - nr * lam_im) / den
    B_re = B_re.astype(f32)
    B_im = B_im.astype(f32)
    bb_re = c_re[..., None] * B_re - c_im[..., None] * B_im
    bb_im = c_re[..., None] * B_im + c_im[..., None] * B_re
    bu_re = jnp.einsum('btgp,gnp->btgn', uf, bb_re)
    bu_im = jnp.einsum('btgp,gnp->btgn', uf, bb_im)
    a_shape = (1, t_len) + ab_re.shape
    a_re = jnp.broadcast_to(ab_re, a_shape)
    a_im = jnp.broadcast_to(ab_im, a_shape)
    _, _, x_re, x_im = lax.associative_scan(complex_combine, (a_re, a_im, bu_re, bu_im), axis=1)
    y = (jnp.einsum('btgn,gpn->btgp', x_re, C_re.astype(f32))
         - jnp.einsum('btgn,gpn->btgp', x_im, C_im.astype(f32))
         + D.astype(f32).reshape(S5_GROUPS, S5_CH) * uf)
    y = jax.nn.gelu(y.reshape(bsz, t_len, width))
    return y * jax.nn.sigmoid(y @ glu_w.astype(f32) + glu_b.astype(f32))


def rglru(xb, gb, conv_w, conv_b, wa, ba, wx, bx, lru_param):
    f32 = jnp.float32
    bsz, t_len, width = xb.shape
    xc = (causal_conv(xb, conv_w) + conv_b).astype(f32)
    xh = xc.reshape(bsz, t_len, LRU_BLOCKS, LRU_BLOCK_DIM)
    r = jax.nn.sigmoid(jnp.einsum('btnd,nde->btne', xh, wa.astype(f32)).reshape(bsz, t_len, width) + ba.astype(f32))
    gi = jax.nn.sigmoid(jnp.einsum('btnd,nde->btne', xh, wx.astype(f32)).reshape(bsz, t_len, width) + bx.astype(f32))
    log_a = -LRU_C * r * jax.nn.softplus(-lru_param.astype(f32))
    a = jnp.exp(log_a)
    mult = jnp.sqrt(-jnp.expm1(2.0 * log_a))
    _, h = lax.associative_scan(real_combine, (a, mult * gi * xc), axis=1)
    return h * jax.nn.gelu(gb.astype(f32))


def token_mix(h, w_in, w_out, gdn_conv_w, gdn_A_log, gdn_dt_bias, gdn_norm_g, hgrn_lb, hgrn_norm_g,
              s5_lam_re, s5_lam_im, s5_log_dt, s5_B_re, s5_B_im, s5_C_re, s5_C_im, s5_D, s5_glu_w, s5_glu_b,
              lru_conv_w, lru_conv_b, lru_wa, lru_ba, lru_wx, lru_bx, lru_param, branch_norm_g):
    bsz, t_len, _ = h.shape
    cuts = [int(c) for c in np.cumsum(IN_SPLIT)[:-1]]
    (a_qkv, a_z, a_beta, a_dec, b_q, b_f, b_i, b_g, c_u, d_x, d_g) = jnp.split(h @ w_in, cuts, axis=-1)

    def heads(t, d):
        return t.reshape(bsz, t_len, -1, d)

    qkv = jax.nn.silu(causal_conv(a_qkv, gdn_conv_w))
    a_q, a_k, a_v = jnp.split(qkv, 3, axis=-1)
    y_a = gated_deltanet(heads(a_q, GDN_HEAD_DIM), heads(a_k, GDN_HEAD_DIM), heads(a_v, GDN_HEAD_DIM),
                         heads(a_z, GDN_HEAD_DIM), a_beta, a_dec, gdn_A_log, gdn_dt_bias, gdn_norm_g)
    y_b = hgrn2(heads(b_q, HGRN_HEAD_DIM), heads(b_f, HGRN_HEAD_DIM), heads(b_i, HGRN_HEAD_DIM),
                heads(b_g, HGRN_HEAD_DIM), hgrn_lb, hgrn_norm_g)
    y_c = rms_norm(s5(c_u, s5_lam_re, s5_lam_im, s5_log_dt, s5_B_re, s5_B_im, s5_C_re, s5_C_im, s5_D,
                      s5_glu_w, s5_glu_b), branch_norm_g[0])
    y_d = rms_norm(rglru(d_x, d_g, lru_conv_w, lru_conv_b, lru_wa, lru_ba, lru_wx, lru_bx, lru_param),
                   branch_norm_g[1])
    y = jnp.concatenate([y_a, y_b, y_c, y_d], axis=-1).astype(h.dtype)
    return y @ w_out


def setup_inputs(seed: int = 0) -> dict:
    key = jax.random.key(seed)
    ks = jax.random.split(key, 40)
    f32 = jnp.float32

    def nrm(k, shape, scale):
        return jax.random.normal(k, shape, f32) * scale

    def unif(k, shape, lo, hi):
        return jax.random.uniform(k, shape, f32, lo, hi)

    W = GROUP_WIDTH
    dt = jnp.exp(unif(ks[9], (DEPTH, GDN_HEADS), np.log(1e-3), np.log(1e-1)))
    a0 = unif(ks[30], (DEPTH, W), 0.9, 0.999)
    sig = a0 ** (1.0 / LRU_C)
    n_idx = jnp.arange(S5_STATE, dtype=f32)
    return {
        'x': nrm(ks[0], (BATCH, SEQ, D_MODEL), 1.0),
        'p': nrm(ks[1], (DEPTH, BATCH, SEQ, PLE_DIM), 1.0),
        'ln_g': 1.0 + nrm(ks[2], (DEPTH, 4, D_MODEL), 0.02),
        'ln_b': nrm(ks[3], (DEPTH, 4, D_MODEL), 0.01),
        'ffn_wi': nrm(ks[4], (DEPTH, 2, D_MODEL, 2 * D_FF), D_MODEL ** -0.5),
        'ffn_wo': nrm(ks[5], (DEPTH, 2, D_FF, D_MODEL), D_FF ** -0.5 * DN_BETA),
        'mix_w_in': nrm(ks[6], (DEPTH, D_MODEL, IN_COLS), D_MODEL ** -0.5),
        'mix_w_out': nrm(ks[7], (DEPTH, D_MODEL, D_MODEL), D_MODEL ** -0.5 * DN_BETA),
        'gdn_conv_w': nrm(ks[8], (DEPTH, CONV_WIDTH, 3 * W), 0.5),
        'gdn_A_log': jnp.log(unif(ks[10], (DEPTH, GDN_HEADS), 1.0, 16.0)),
        'gdn_dt_bias': dt + jnp.log(-jnp.expm1(-dt)),
        'gdn_norm_g': 1.0 + nrm(ks[11], (DEPTH, GDN_HEAD_DIM), 0.02),
        'hgrn_lb_logits': nrm(ks[12], (DEPTH, W), 0.5),
        'hgrn_norm_g': 1.0 + nrm(ks[13], (DEPTH, HGRN_HEAD_DIM), 0.02),
        's5_lam_re': -0.5 + nrm(ks[14], (DEPTH, S5_GROUPS, S5_STATE), 0.01),
        's5_lam_im': np.pi * n_idx + nrm(ks[15], (DEPTH, S5_GROUPS, S5_STATE), 0.01),
        's5_log_dt': unif(ks[16], (DEPTH, S5_GROUPS), np.log(1e-3), np.log(1e-1)),
        's5_B_re': nrm(ks[17], (DEPTH, S5_GROUPS, S5_STATE, S5_CH), (2 * S5_CH) ** -0.5),
        's5_B_im': nrm(ks[18], (DEPTH, S5_GROUPS, S5_STATE, S5_CH), (2 * S5_CH) ** -0.5),
        's5_C_re': nrm(ks[19], (DEPTH, S5_GROUPS, S5_CH, S5_STATE), (2 * S5_STATE) ** -0.5),
        's5_C_im': nrm(ks[20], (DEPTH, S5_GROUPS, S5_CH, S5_STATE), (2 * S5_STATE) ** -0.5),
        's5_D': nrm(ks[21], (DEPTH, W), 1.0),
        's5_glu_w': nrm(ks[22], (DEPTH, W, W), W ** -0.5),
        's5_glu_b': nrm(ks[23], (DEPTH, W), 0.01),
        'lru_conv_w': nrm(ks[24], (DEPTH, CONV_WIDTH, W), 0.5),
        'lru_conv_b': nrm(ks[25], (DEPTH, W), 0.01),
        'lru_wa': nrm(ks[26], (DEPTH, LRU_BLOCKS, LRU_BLOCK_DIM, LRU_BLOCK_DIM), LRU_BLOCK_DIM ** -0.5),
        'lru_ba': nrm(ks[27], (DEPTH, W), 0.01),
        'lru_wx': nrm(ks[28], (DEPTH, LRU_BLOCKS, LRU_BLOCK_DIM, LRU_BLOCK_DIM), LRU_BLOCK_DIM ** -0.5),
        'lru_bx': nrm(ks[29], (DEPTH, W), 0.01),
        'lru_param': jnp.log(sig) - jnp.log1p(-sig),
        'branch_norm_g': 1.0 + nrm(ks[31], (DEPTH, 2, W), 0.02),
        'ple_w': nrm(ks[32], (DEPTH, PLE_DIM, D_MODEL), PLE_DIM ** -0.5 * DN_BETA),
        'ple_gate_w': nrm(ks[33], (DEPTH, D_MODEL, D_MODEL), D_MODEL ** -0.5),
    }


def reference(x, p, ln_g, ln_b, ffn_wi, ffn_wo, mix_w_in, mix_w_out, gdn_conv_w, gdn_A_log, gdn_dt_bias,
              gdn_norm_g, hgrn_lb_logits, hgrn_norm_g, s5_lam_re, s5_lam_im, s5_log_dt, s5_B_re, s5_B_im,
              s5_C_re, s5_C_im, s5_D, s5_glu_w, s5_glu_b, lru_conv_w, lru_conv_b, lru_wa, lru_ba, lru_wx,
              lru_bx, lru_param, branch_norm_g, ple_w, ple_gate_w):
    lb_cum = jnp.cumsum(jax.nn.softmax(hgrn_lb_logits.astype(jnp.float32), axis=0), axis=0)
    lower_bounds = lb_cum - lb_cum[0:1]
    for i in range(DEPTH):
        x = layer_norm(DN_ALPHA * x + 0.5 * swiglu(x, ffn_wi[i, 0], ffn_wo[i, 0]), ln_g[i, 0], ln_b[i, 0])
        mix = token_mix(x, mix_w_in[i], mix_w_out[i], gdn_conv_w[i], gdn_A_log[i], gdn_dt_bias[i], gdn_norm_g[i],
                        lower_bounds[i].reshape(HGRN_HEADS, HGRN_HEAD_DIM), hgrn_norm_g[i],
                        s5_lam_re[i], s5_lam_im[i], s5_log_dt[i], s5_B_re[i], s5_B_im[i], s5_C_re[i], s5_C_im[i],
                        s5_D[i], s5_glu_w[i], s5_glu_b[i], lru_conv_w[i], lru_conv_b[i], lru_wa[i], lru_ba[i],
                        lru_wx[i], lru_bx[i], lru_param[i], branch_norm_g[i])
        x = layer_norm(DN_ALPHA * x + mix, ln_g[i, 1], ln_b[i, 1])
        x = layer_norm(DN_ALPHA * x + 0.5 * swiglu(x, ffn_wi[i, 1], ffn_wo[i, 1]), ln_g[i, 2], ln_b[i, 2])
        ple = (p[i] @ ple_w[i]) * jax.nn.sigmoid(x @ ple_gate_w[i])
        x = layer_norm(DN_ALPHA * x + ple, ln_g[i, 3], ln_b[i, 3])
    return x
```

```python
from contextlib import ExitStack
import numpy as np
import concourse.bass as bass
import concourse.mybir as mybir
from concourse.bass_utils import run_bass_kernel_spmd

F32 = mybir.dt.float32
BF16 = mybir.dt.bfloat16
AF = mybir.ActivationFunctionType
ALU = mybir.AluOpType
DTSIZE = {F32: 4, BF16: 2}

ENGS = ("pe", "act", "dve", "pool", "sp")
SEM_LIMIT = 30000
N_DMA_SEMS = 24

D = 2048
DFF = 5632
W_ = 512
INC = 5640
TB = 512
NCH = 16
ALPHA = 4 ** 0.25
LN_EPS = 1e-5
RMS_EPS = 1e-6


def _region(ap):
    sp = str(ap.space)
    if "SB" not in sp and "PSUM" not in sp:
        if ap.name.startswith("cc_"):
            return (ap.name, 0, 1, 0, 1)
        return None
    pairs = ap.ap
    sz = DTSIZE[ap.dtype]
    pstep, pcount = pairs[0]
    off = ap.offset
    if pstep > 0:
        p0 = off // pstep
        f0 = off % pstep
    else:
        p0 = 0
        f0 = off
    ext = 1
    for s, c in pairs[1:]:
        if s > 0:
            ext += (c - 1) * s
    if "PSUM" in sp:
        return (ap.name, 0, 128, 0, 2048)
    return (ap.name, p0, p0 + pcount, f0 * sz, (f0 + ext) * sz)


class Prog:
    def __init__(self, nc):
        self.nc = nc
        self.streams = {e: [] for e in ENGS}
        self.cnt = {e: 0 for e in ENGS}
        self.epoch = {e: 0 for e in ENGS}
        self.synced = {e: {} for e in ENGS}
        self.hist = {}
        self.dma_uses = [0] * N_DMA_SEMS
        self.dma_rr = 0
        self.dma_rrq = [0, 0]
        self.cc_uses = 0
        self.sem_keys = set()
        self.nops = 0

    def _deps(self, eng, reads, writes):
        waits = {}
        syn = self.synced[eng]
        regs = []
        for ap in reads:
            r = _region(ap)
            if r is not None:
                regs.append((r, r[0].startswith("ps")))
        for ap in writes:
            r = _region(ap)
            if r is not None:
                regs.append((r, True))
        for (name, p0, p1, b0, b1), isw in regs:
            lst = self.hist.get(name)
            if not lst:
                continue
            for rec in lst:
                key, val, rw, q0, q1, c0, c1 = rec
                if not (isw or rw):
                    continue
                if q1 <= p0 or p1 <= q0 or c1 <= b0 or b1 <= c0:
                    continue
                if key[0] == "pe" and eng == "pe":
                    continue
                if syn.get(key, 0) >= val:
                    continue
                if waits.get(key, 0) < val:
                    waits[key] = val
        for k, v in waits.items():
            syn[k] = v
        return list(waits.items()), regs

    def _record(self, regs, key, val):
        for (name, p0, p1, b0, b1), isw in regs:
            lst = self.hist.setdefault(name, [])
            if isw:
                lst[:] = [r for r in lst if not (r[3] >= p0 and r[4] <= p1 and r[5] >= b0 and r[6] <= b1)]
            else:
                lst[:] = [r for r in lst if not ((not r[2]) and r[0] == key and r[3] >= p0 and r[4] <= p1
                                                 and r[5] >= b0 and r[6] <= b1)]
            lst.append((key, val, isw, p0, p1, b0, b1))

    def op(self, eng, fn, reads=(), writes=()):
        waits, regs = self._deps(eng, reads, writes)
        if self.cnt[eng] >= SEM_LIMIT:
            self.epoch[eng] += 1
            self.cnt[eng] = 0
        self.cnt[eng] += 1
        key = (eng, self.epoch[eng])
        val = self.cnt[eng]
        self.sem_keys.add(key)
        self.streams[eng].append((waits, fn, (key, 1)))
        self._record(regs, key, val)
        if eng == "pe":
            self.synced[eng][key] = val
        self.nops += 1

    def dma(self, queue, out, in_, **kw):
        waits, regs = self._deps(queue, [in_], [out])
        half = N_DMA_SEMS // 2
        qi = 0 if queue == "sp" else 1
        j = qi * half + self.dma_rrq[qi]
        self.dma_rrq[qi] = (self.dma_rrq[qi] + 1) % half
        key = ("dma", j)
        prev = self.dma_uses[j] * 16
        if prev and self.synced[queue].get(key, 0) < prev:
            waits.append((key, prev))
            self.synced[queue][key] = prev
        self.dma_uses[j] += 1
        val = self.dma_uses[j] * 16
        self.sem_keys.add(key)
        self.streams[queue].append((waits, lambda e: e.dma_start(out=out, in_=in_, **kw), (key, 16)))
        self._record(regs, key, val)
        self.nops += 1

    def cc(self, groups, in_ap, out_ap):
        waits, regs = self._deps("pool", [in_ap], [out_ap])
        self.cc_uses += 1
        key = ("cc", 0)
        self.sem_keys.add(key)
        self.streams["pool"].append((waits, lambda e: e.collective_compute(
            "AllGather", ALU.bypass, replica_groups=groups, ins=[in_ap], outs=[out_ap]), (key, 1)))
        self._record(regs, key, self.cc_uses)
        self.nops += 1

    def wait_all_dma(self, queue="sp"):
        waits = []
        for j in range(N_DMA_SEMS):
            if self.dma_uses[j]:
                waits.append((("dma", j), self.dma_uses[j] * 16))
        self.streams[queue].append((waits, None, None))

    def mm(self, out, lhsT, rhs, start=True, stop=True):
        self.op("pe", lambda e: e.matmul(out, lhsT, rhs, start=start, stop=stop), [lhsT, rhs], [out])

    def tr(self, out, in_, ident):
        self.op("pe", lambda e: e.transpose(out, in_, ident), [in_, ident], [out])

    def act(self, out, in_, func, bias=None, scale=1.0):
        rd = [in_]
        kw = {}
        if bias is not None:
            kw["bias"] = bias
            if not isinstance(bias, (int, float)):
                rd.append(bias)
        if not isinstance(scale, (int, float)):
            rd.append(scale)
        self.op("act", lambda e: e.activation(out, in_, func, scale=scale, **kw), rd, [out])

    def tt(self, out, a, b, op, eng="dve"):
        self.op(eng, lambda e: e.tensor_tensor(out, a, b, op), [a, b], [out])

    def ts(self, out, a, s1, s2, op0, op1=None, eng="dve"):
        rd = [a]
        for s in (s1, s2):
            if s is not None and not isinstance(s, (int, float)):
                rd.append(s)
        if op1 is None:
            self.op(eng, lambda e: e.tensor_scalar(out, a, s1, None, op0), rd, [out])
        else:
            self.op(eng, lambda e: e.tensor_scalar(out, a, s1, s2, op0, op1), rd, [out])

    def stt(self, out, a, s, b, op0, op1):
        rd = [a, b]
        if not isinstance(s, (int, float)):
            rd.append(s)
        self.op("dve", lambda e: e.scalar_tensor_tensor(out, a, s, b, op0, op1), rd, [out])

    def copy(self, out, in_, eng="dve"):
        if eng == "act":
            self.op("act", lambda e: e.copy(out, in_), [in_], [out])
        else:
            self.op(eng, lambda e: e.tensor_copy(out, in_), [in_], [out])

    def memset(self, out, v, eng="dve"):
        self.op(eng, lambda e: e.memset(out, v), [], [out])

    def recip(self, out, in_):
        self.op("dve", lambda e: e.reciprocal(out, in_), [in_], [out])

    def scan(self, out, d0, d1, init):
        rd = [d0, d1]
        if not isinstance(init, (int, float)):
            rd.append(init)
        self.op("dve", lambda e: e.tensor_tensor_scan(out, d0, d1, init, ALU.mult, ALU.add), rd, [out])

    def emit(self, stack):
        nc = self.nc
        sems = {}
        for key in sorted(self.sem_keys, key=str):
            sems[key] = stack.enter_context(nc.semaphore("s_%s_%d" % key))
        block = stack.enter_context(nc.Block())

        def run(stream):
            def body(eng):
                for waits, fn, inc in stream:
                    for k, v in waits:
                        eng.wait_ge(sems[k], v)
                    if fn is None:
                        continue
                    ins = fn(eng)
                    if inc is not None:
                        ins.then_inc(sems[inc[0]], inc[1])
            return body

        block.tensor(run(self.streams["pe"]))
        block.scalar(run(self.streams["act"]))
        block.vector(run(self.streams["dve"]))
        block.gpsimd(run(self.streams["pool"]))
        block.sync(run(self.streams["sp"]))


class VecPack:
    def __init__(self):
        self.cols = []
        self.index = {}

    def add(self, name, arr2d):
        a = np.zeros((128, arr2d.shape[1]), np.float32)
        a[: arr2d.shape[0]] = arr2d
        self.index[name] = sum(c.shape[1] for c in self.cols)
        self.cols.append(a)

    def build(self):
        return np.ascontiguousarray(np.concatenate(self.cols, axis=1))


def chunkcols(v):
    return np.ascontiguousarray(v.reshape(-1, 128).T)


def build_vec(inp, depth):
    vp = VecPack()
    for l in range(depth):
        vp.add("ln_g%d" % l, np.concatenate([chunkcols(inp["ln_g"][l, s]) for s in range(4)], axis=1))
        vp.add("ln_b%d" % l, np.concatenate([chunkcols(inp["ln_b"][l, s]) for s in range(4)], axis=1))
        cw = inp["gdn_conv_w"][l]
        vp.add("gconv%d" % l, np.stack([chunkcols(cw[j]) for j in range(4)], axis=2).reshape(128, 48))
        vp.add("gnorm%d" % l, inp["gdn_norm_g"][l].reshape(128, 1))
        vp.add("hnorm%d" % l, inp["hgrn_norm_g"][l].reshape(128, 1))
        vp.add("hlb%d" % l, chunkcols(inp["hgrn_lb_logits"][l]))
        vp.add("lam_re%d" % l, chunkcols(inp["s5_lam_re"][l].reshape(-1)))
        vp.add("lam_im%d" % l, chunkcols(inp["s5_lam_im"][l].reshape(-1)))
        vp.add("logdt%d" % l, chunkcols(np.repeat(inp["s5_log_dt"][l], 64)))
        vp.add("s5D%d" % l, chunkcols(inp["s5_D"][l]))
        vp.add("glub%d" % l, chunkcols(inp["s5_glu_b"][l]))
        lw = inp["lru_conv_w"][l]
        vp.add("lconv%d" % l, np.stack([chunkcols(lw[j]) for j in range(4)], axis=2).reshape(128, 16))
        vp.add("lconvb%d" % l, chunkcols(inp["lru_conv_b"][l]))
        vp.add("lba%d" % l, chunkcols(inp["lru_ba"][l]))
        vp.add("lbx%d" % l, chunkcols(inp["lru_bx"][l]))
        vp.add("lparam%d" % l, chunkcols(inp["lru_param"][l]))
        vp.add("bng%d" % l, np.concatenate([chunkcols(inp["branch_norm_g"][l, s]) for s in range(2)], axis=1))
        vp.add("alog%d" % l, inp["gdn_A_log"][l].reshape(4, 1))
        vp.add("dtb%d" % l, inp["gdn_dt_bias"][l].reshape(4, 1))
    return vp


def build_consts():
    c = {}
    c["ident"] = np.eye(128, dtype=np.float32)
    c["ones_d"] = np.full((128, 128), 1.0 / D, np.float32)
    c["ones_w"] = np.full((128, 128), 1.0 / W_, np.float32)
    c["ones_1"] = np.full((128, 128), 1.0, np.float32)
    c["ones_h"] = np.full((128, 128), 1.0 / 128, np.float32)
    i = np.arange(64)
    NEG = -1e30
    c["m_strict_add"] = np.where(i[:, None] > i[None, :], 0.0, NEG).astype(np.float32)
    c["m_inclT_add"] = np.where(i[:, None] <= i[None, :], 0.0, NEG).astype(np.float32)
    c["m_strictT_01"] = (i[:, None] < i[None, :]).astype(np.float32)
    c["m_inclT_01"] = (i[:, None] <= i[None, :]).astype(np.float32)
    m64 = np.ones((128, TB), np.float32)
    m64[:, ::64] = 0.0
    c["start64"] = m64
    m32 = np.ones((128, TB), np.float32)
    m32[:, ::32] = 0.0
    c["start32"] = m32
    sel = np.zeros((4, 4, 128), np.float32)
    for h in range(4):
        sel[h, h, :] = 1.0
    c["sel"] = sel.reshape(4, 512)
    c["nsel"] = np.where(sel != 0, -1.0, 0.0).astype(np.float32).reshape(4, 512)
    return c


def build_mats(inp, l):
    Bre, Bim = inp["s5_B_re"][l], inp["s5_B_im"][l]
    Cre, Cim = inp["s5_C_re"][l], inp["s5_C_im"][l]
    Bp = np.zeros((128, 2, 16, 128), np.float32)
    Cp = np.zeros((128, 2, 16, 128), np.float32)
    for j in range(16):
        for gi in range(2):
            g = 2 * j + gi
            lg = g % 8
            Bp[16 * lg:16 * lg + 16, 0, j, 64 * gi:64 * gi + 64] = Bre[g].T
            Bp[16 * lg:16 * lg + 16, 1, j, 64 * gi:64 * gi + 64] = Bim[g].T
            Cp[64 * gi:64 * gi + 64, 0, j, 16 * lg:16 * lg + 16] = Cre[g].T
            Cp[64 * gi:64 * gi + 64, 1, j, 16 * lg:16 * lg + 16] = Cim[g].T
    glu = np.ascontiguousarray(inp["s5_glu_w"][l].reshape(4, 128, 512).transpose(1, 0, 2))
    Wb = np.zeros((128, 2, 4, 128), np.float32)
    for i in range(4):
        for bi in range(2):
            n = 2 * i + bi
            Wb[64 * bi:64 * bi + 64, 0, i, 64 * bi:64 * bi + 64] = inp["lru_wa"][l, n]
            Wb[64 * bi:64 * bi + 64, 1, i, 64 * bi:64 * bi + 64] = inp["lru_wx"][l, n]
    return Bp, Cp, glu, Wb


def build_program(T, depth, vidx, nvec, dbg=None, stages=None, pipe=False):
    ST = stages if stages is not None else {'derived', 'ffn1', 'gdn', 'hgrn', 's5', 'lru', 'oproj', 'ffn2', 'ple'}
    NB = T // TB
    NST = NB + 1 if pipe else NB
    TT = NST * TB
    nc = bass.Bass("TRN2", target_bir_lowering=False)

    def din(name, shape):
        return nc.dram_tensor(name, list(shape), F32, kind="ExternalInput").ap()

    xT = din("xT", [D, TT])
    pT = din("pT", [depth, 256, TT])
    cmask_d = din("cmask", [128, 4])
    wi = din("ffn_wi", [depth, 2, D, 2 * DFF])
    wo = din("ffn_wo", [depth, 2, DFF, D])
    w_in = din("mix_w_in", [depth, D, INC])
    w_out = din("mix_w_out", [depth, D, D])
    ple_w = din("ple_w", [depth, 256, D])
    ple_gw = din("ple_gate_w", [depth, D, D])
    vec_d = din("vec", [128, nvec])
    cst = {k: din("c_" + k, v.shape) for k, v in build_consts().items()}
    Bp_d = din("Bp", [depth, 128, 2, 16, 128])
    Cp_d = din("Cp", [depth, 128, 2, 16, 128])
    glu_d = din("glu", [depth, 128, 4, 512])
    Wb_d = din("Wb", [depth, 128, 2, 4, 128])
    oT = nc.dram_tensor("oT", [D, TT], F32, kind="ExternalOutput").ap()
    cc_send = [nc.dram_tensor("cc_send%d" % j, [1024, TB], F32) for j in range(2)] if pipe else None
    cc_recv = [nc.dram_tensor("cc_recv%d" % j, [2048, TB], F32) for j in range(2)] if pipe else None
    dbg_d = None
    if dbg is not None:
        dbg_d = nc.dram_tensor("dbg", [128, dbg, TB], F32, kind="ExternalOutput").ap()

    st = ExitStack()

    def sb(name, shape, dt=F32):
        return st.enter_context(nc.sbuf_tensor("s_" + name, list(shape), dt))

    P = Prog(nc)
    x = sb("x", [128, NCH, TB])
    xb = sb("xb", [128, NCH, TB], BF16)
    NSC = 24
    sc = sb("sc", [128, NSC, TB])
    NWR = 2
    wring = [sb("wr%d" % i, [128, 16, 512], BF16) for i in range(NWR)]
    vec = sb("vec", [128, nvec])
    cmask = sb("cmask", [128, 4])
    cs_ = {k: sb("k_" + k, v.shape) for k, v in build_consts().items()}
    ybuf = sb("ybuf", [128, NCH, TB], BF16)
    tabc = sb("tabc", [128, 16, 256])
    tabs = sb("tabs", [128, 16, 256])
    NDC = 96
    dcol = sb("dcol", [128, depth, NDC])
    gS = sb("gS", [128, depth, 4, 128])
    hS = sb("hS", [128, depth, 4, 128])
    s5st = sb("s5st", [128, depth, 16, 2])
    lrust = sb("lrust", [128, depth, 4])
    ctail = sb("ctail", [128, depth, 16, 3])
    pb = sb("pb", [128, 2, TB], BF16)
    ps = [st.enter_context(nc.psum_tensor("ps%d" % i, [128, TB], F32)) for i in range(8)]
    psc = [0]

    def nps():
        psc[0] = (psc[0] + 1) % 6
        return ps[psc[0]]

    ident = cs_["ident"]
    wslot = [0]

    def V(name, l=None):
        return vidx[name if l is None else "%s%d" % (name, l)]

    def vcol(name, l, c):
        o = V(name, l) + c
        return vec[:, o:o + 1]

    P.dma("sp", vec[:], vec_d)
    P.dma("sp", cmask[:], cmask_d)
    for k in cst:
        P.dma("sp", cs_[k][:], cst[k])
    for t_ in (gS, hS, s5st, lrust, ctail):
        P.memset(t_[:], 0.0)

    MAGIC = 12582912.0

    def sincos(ang, out_sin, out_cos, t_a, t_b):
        for shift, dst in ((0.0, out_sin), (float(0.5 * np.pi), out_cos)):
            P.ts(t_a, ang, shift, float(1.0 / (2 * np.pi)), ALU.add, ALU.mult)
            P.ts(t_b, t_a, MAGIC, None, ALU.add)
            P.ts(t_b, t_b, -MAGIC, None, ALU.add)
            P.tt(t_a, t_a, t_b, ALU.subtract)
            P.act(dst, t_a, AF.Sin, scale=float(2 * np.pi))

    tmpc = sc[:, 0, :]
    for l in range(depth if 'derived' in ST else 0):
        dc = dcol[:, l, :]
        lre = vec[:, V("lam_re", l):V("lam_re", l) + 16]
        lim = vec[:, V("lam_im", l):V("lam_im", l) + 16]
        ldt = vec[:, V("logdt", l):V("logdt", l) + 16]
        dt = tmpc[:, 0:16]
        P.act(dt, ldt, AF.Exp)
        ex = tmpc[:, 16:32]
        P.tt(ex, lre, dt, ALU.mult)
        ang = tmpc[:, 32:48]
        P.tt(ang, lim, dt, ALU.mult)
        P.act(dc[:, 0:16], ex, AF.Exp)
        angr = tmpc[:, 48:64]
        sincos(ang, tmpc[:, 64:80], tmpc[:, 96:112], angr, tmpc[:, 80:96])
        sn = tmpc[:, 64:80]
        cs = tmpc[:, 96:112]
        abre = tmpc[:, 112:128]
        abim = tmpc[:, 128:144]
        P.tt(abre, dc[:, 0:16], cs, ALU.mult)
        P.tt(abim, dc[:, 0:16], sn, ALU.mult)
        den = tmpc[:, 144:160]
        t1 = tmpc[:, 160:176]
        P.tt(den, lre, lre, ALU.mult)
        P.tt(t1, lim, lim, ALU.mult)
        P.tt(den, den, t1, ALU.add)
        rden = tmpc[:, 176:192]
        P.recip(rden, den)
        nr = tmpc[:, 192:208]
        P.ts(nr, abre, -1.0, None, ALU.add)
        t2 = tmpc[:, 208:224]
        P.tt(t1, nr, lre, ALU.mult)
        P.tt(t2, abim, lim, ALU.mult)
        P.tt(t1, t1, t2, ALU.add)
        P.tt(dc[:, 16:32], t1, rden, ALU.mult)
        P.tt(t1, abim, lre, ALU.mult)
        P.tt(t2, nr, lim, ALU.mult)
        P.tt(t1, t1, t2, ALU.subtract)
        P.tt(dc[:, 32:48], t1, rden, ALU.mult)
        P.ts(dc[:, 48:64], dc[:, 32:48], -1.0, None, ALU.mult)
        lp = vec[:, V("lparam", l):V("lparam", l) + 4]
        t3 = tmpc[:, 224:228]
        P.act(t3, lp, AF.Exp, scale=-1.0)
        P.act(t3, t3, AF.Ln, bias=1.0)
        P.ts(dc[:, 64:68], t3, -8.0, None, ALU.mult)
        P.ts(dc[:, 68:72], t3, -16.0, None, ALU.mult)
        if pipe:
            l0 = vec[:, V("hlbA"):V("hlbA") + 4]
            l1 = vec[:, V("hlbB"):V("hlbB") + 4]
            t4 = tmpc[:, 228:232]
            P.tt(t4, l1, l0, ALU.subtract)
            P.act(t4, t4, AF.Sigmoid)
            P.ts(dc[:, 72:76], t4, cmask[:, 1:2], None, ALU.mult)
        elif l == 0:
            P.memset(dc[:, 72:76], 0.0)
        else:
            l0 = vec[:, V("hlb", 0):V("hlb", 0) + 4]
            l1 = vec[:, V("hlb", 1):V("hlb", 1) + 4]
            t4 = tmpc[:, 228:232]
            P.tt(t4, l1, l0, ALU.subtract)
            P.act(dc[:, 72:76], t4, AF.Sigmoid)
        P.ts(dc[:, 76:80], dc[:, 72:76], -1.0, 1.0, ALU.mult, ALU.add)
        al = vec[0:4, V("alog", l):V("alog", l) + 1]
        P.act(dc[0:4, 80:81], al, AF.Exp)
        P.ts(dc[0:4, 80:81], dc[0:4, 80:81], -1.0, None, ALU.mult)
    seeds = sb("seeds", [128, depth, 2, 16])
    for l in range(depth if 'derived' in ST else 0):
        lim = vec[:, V("lam_im", l):V("lam_im", l) + 16]
        ldt = vec[:, V("logdt", l):V("logdt", l) + 16]
        dt = tmpc[:, 0:16]
        P.act(dt, ldt, AF.Exp)
        ang = tmpc[:, 32:48]
        P.tt(ang, lim, dt, ALU.mult)
        sincos(ang, seeds[:, l, 1, :], seeds[:, l, 0, :], tmpc[:, 48:64], tmpc[:, 80:96])

    cur_tab_layer = [None]

    def build_tables(l):
        if cur_tab_layer[0] == l:
            return
        cur_tab_layer[0] = l
        P.copy(tabc[:, :, 0:1], seeds[:, l, 0, :].unsqueeze(2))
        P.copy(tabs[:, :, 0:1], seeds[:, l, 1, :].unsqueeze(2))
        k = 1
        t1 = sc[:, 0:4, :].rearrange("p a b -> p (a b)")[:, 0:16 * 128].rearrange("p (a b) -> p a b", b=128)
        t2 = sc[:, 4:8, :].rearrange("p a b -> p (a b)")[:, 0:16 * 128].rearrange("p (a b) -> p a b", b=128)
        while k < 256:
            ck = tabc[:, :, k - 1:k].to_broadcast([128, 16, k])
            sk = tabs[:, :, k - 1:k].to_broadcast([128, 16, k])
            c0 = tabc[:, :, 0:k]
            s0 = tabs[:, :, 0:k]
            a1 = t1[:, :, 0:k]
            a2 = t2[:, :, 0:k]
            P.tt(a1, c0, ck, ALU.mult)
            P.tt(a2, s0, sk, ALU.mult)
            P.tt(tabc[:, :, k:2 * k], a1, a2, ALU.subtract)
            P.tt(a1, s0, ck, ALU.mult)
            P.tt(a2, c0, sk, ALU.mult)
            P.tt(tabs[:, :, k:2 * k], a1, a2, ALU.add)
            k *= 2

    def wload(srcs):
        slot = wring[wslot[0] % NWR]
        wslot[0] += 1
        for ap, k0, n0 in srcs:
            kk, nn = ap.shape[1], ap.shape[2]
            P.dma("pool", slot[:, k0:k0 + kk, n0:n0 + nn], ap)
        return slot

    def wview(w2d):
        return w2d.rearrange("(k p) n -> p k n", p=128)

    def dbg_out(i, ap):
        if dbg_d is not None:
            P.dma("sp", dbg_d[:, i, :], ap)

    def layer_norm(l, s):
        p1 = nps()
        p2 = nps()
        sq = sc[:, 2:4, :]
        for c in range(NCH):
            P.act(sq[:, c % 2, :], x[:, c, :], AF.Square)
            P.mm(p1[:], cs_["ones_d"][:], x[:, c, :], start=(c == 0), stop=(c == NCH - 1))
            P.mm(p2[:], cs_["ones_d"][:], sq[:, c % 2, :], start=(c == 0), stop=(c == NCH - 1))
        mean = sc[:, 0, :]
        rstd = sc[:, 1, :]
        P.copy(mean, p1[:], eng="act")
        var = sc[:, 2, :]
        P.tt(var, mean, mean, ALU.mult)
        P.tt(var, p2[:], var, ALU.subtract)
        P.act(var, var, AF.Sqrt, bias=LN_EPS)
        P.recip(rstd, var)
        for c in range(NCH):
            t = sc[:, 2 + (c % 2), :]
            P.tt(t, x[:, c, :], mean, ALU.subtract)
            P.tt(t, t, rstd, ALU.mult)
            g = vcol("ln_g", l, s * 16 + c)
            b = vcol("ln_b", l, s * 16 + c)
            P.act(x[:, c, :], t, AF.Identity, bias=b, scale=g)
            P.copy(xb[:, c, :], x[:, c, :], eng="act" if c % 2 else "dve")

    def resid(c, psum_ap, scale):
        t = sc[:, 22 + (c % 2), :]
        P.act(t, psum_ap, AF.Copy, scale=scale)
        P.stt(x[:, c, :], x[:, c, :], ALU_ALPHA, t, ALU.mult, ALU.add)

    ALU_ALPHA = float(ALPHA)

    def ffn(l, f, s):
        h = sc[:, 0:22, :].bitcast(BF16).rearrange("p a (b c) -> p (a b) c", c=TB)
        wiv = wview(wi[l, f])
        sil = sc[:, 22:24, :].bitcast(BF16).rearrange("p a (b c) -> p (a b) c", c=TB)
        for jg in range(11):
            sg = wload([(wiv[:, :, jg * 512:(jg + 1) * 512], 0, 0)])
            pgs = [nps() for _ in range(4)]
            for jj in range(4):
                for k in range(NCH):
                    P.mm(pgs[jj][:], sg[:, k, jj * 128:(jj + 1) * 128], xb[:, k, :], start=(k == 0), stop=(k == NCH - 1))
            for jj in range(4):
                P.act(sil[:, jj, :], pgs[jj][:], AF.Silu)
            su = wload([(wiv[:, :, DFF + jg * 512:DFF + (jg + 1) * 512], 0, 0)])
            pus = [nps() for _ in range(4)]
            for jj in range(4):
                for k in range(NCH):
                    P.mm(pus[jj][:], su[:, k, jj * 128:(jj + 1) * 128], xb[:, k, :], start=(k == 0), stop=(k == NCH - 1))
            for jj in range(4):
                P.tt(h[:, jg * 4 + jj, :], sil[:, jj, :], pus[jj][:], ALU.mult)
        wov = wview(wo[l, f])
        for og in range(4):
            pss = [nps() for _ in range(4)]
            for kg, (k0, kn) in enumerate(((0, 16), (16, 16), (32, 12))):
                sl = wload([(wov[:, k0:k0 + kn, og * 512:(og + 1) * 512], 0, 0)])
                for oc in range(4):
                    for k in range(kn):
                        P.mm(pss[oc][:], sl[:, k, oc * 128:(oc + 1) * 128], h[:, k0 + k, :],
                             start=(k0 + k == 0), stop=(k0 + k == 43))
            for oc in range(4):
                resid(og * 4 + oc, pss[oc][:], 0.5)
        layer_norm(l, s)

    def ple(l, blk):
        pv = pT[l].rearrange("(k p) t -> p k t", p=128)
        P.dma("pool", pb[:], pv[:, :, blk * TB:(blk + 1) * TB])
        gv = wview(ple_gw[l])
        pwv = wview(ple_w[l])
        sgt = sc[:, 0:4, :]
        pwb = sc[:, 4:8, :].bitcast(BF16).rearrange("p a (b c) -> p (a b) c", c=512).rearrange("p (k g) c -> p k (g c)", k=2)
        P.dma("pool", pwb, pwv)
        for og in range(4):
            sg = wload([(gv[:, :, og * 512:(og + 1) * 512], 0, 0)])
            pgs = [nps() for _ in range(4)]
            for oc in range(4):
                for k in range(NCH):
                    P.mm(pgs[oc][:], sg[:, k, oc * 128:(oc + 1) * 128], xb[:, k, :], start=(k == 0), stop=(k == NCH - 1))
            for oc in range(4):
                P.act(sgt[:, oc, :], pgs[oc][:], AF.Sigmoid)
            pps = [nps() for _ in range(4)]
            for oc in range(4):
                c = og * 4 + oc
                for k in range(2):
                    P.mm(pps[oc][:], pwb[:, k, c * 128:(c + 1) * 128], pb[:, k, :], start=(k == 0), stop=(k == 1))
            for oc in range(4):
                c = og * 4 + oc
                P.tt(sgt[:, oc, :], sgt[:, oc, :], pps[oc][:], ALU.mult)
                P.stt(x[:, c, :], x[:, c, :], ALU_ALPHA, sgt[:, oc, :], ALU.mult, ALU.add)
        layer_norm(l, 3)

    def inproj(l, colstarts, widths, consume):
        wv = wview(w_in[l])
        i = 0
        while i < len(colstarts):
            grp = list(range(i, min(i + 4, len(colstarts))))
            srcs = []
            for gi, ci in enumerate(grp):
                srcs.append((wv[:, :, colstarts[ci]:colstarts[ci] + widths[ci]], 0, gi * 128))
            sl = wload(srcs)
            for gi, ci in enumerate(grp):
                p_ = nps()
                wd = widths[ci]
                for k in range(NCH):
                    P.mm(p_[0:wd, :], sl[:, k, gi * 128:gi * 128 + wd], xb[:, k, :], start=(k == 0), stop=(k == NCH - 1))
                consume(ci, p_)
            i += 4

    def conv4(l, name, chunk, tail_idx, src_ps, out_ap, cw, bias=None):
        P.copy(cw[:, 0:3], ctail[:, l, tail_idx, :], eng="act")
        P.copy(cw[:, 3:3 + TB], src_ps, eng="act")
        o = V(name, l) + chunk * 4
        kw = {}
        P.act(out_ap, cw[:, 0:TB], AF.Identity, scale=vec[:, o:o + 1], bias=bias if bias is not None else 0.0)
        for j in (1, 2, 3):
            P.stt(out_ap, cw[:, j:j + TB], vec[:, o + j:o + j + 1], out_ap, ALU.mult, ALU.add)
        P.copy(ctail[:, l, tail_idx, :], cw[:, TB:TB + 3], eng="act")

    def rms_feat(ones, src_list, eps, tmp_sq, out_rstd):
        p_ = nps()
        n = len(src_list)
        for i, s_ in enumerate(src_list):
            P.act(tmp_sq[:, i % 2, :], s_, AF.Square)
            P.mm(p_[:], ones[:], tmp_sq[:, i % 2, :], start=(i == 0), stop=(i == n - 1))
        P.act(out_rstd, p_[:], AF.Sqrt, bias=eps)
        P.recip(out_rstd, out_rstd)

    def gdn(l):
        dc = dcol[:, l, :]
        rows = sc[0:4, 20, :]
        RB = sc[0:4, 12, :]
        RG = sc[0:4, 13, :]
        REG = sc[0:4, 14, :]
        RBE = sc[0:4, 15, :]
        REL = sc[0:4, 16, :]

        def cons_bd(ci, p_):
            if ci == 0:
                P.act(RB, p_[0:4, :], AF.Sigmoid)
            else:
                dtb = vec[0:4, V("dtb", l):V("dtb", l) + 1]
                P.act(rows, p_[0:4, :], AF.Exp, bias=dtb)
                P.act(rows, rows, AF.Ln, bias=1.0)
                P.ts(rows, rows, dc[0:4, 80:81], None, ALU.mult)
                P.scan(RG, cs_["start64"][0:4, :], rows, 0.0)
                P.act(REG, RG, AF.Exp)
                P.tt(RBE, RB, REG, ALU.mult)
                g3 = RG.rearrange("p (a b) -> p a b", b=64)
                P.tt(REL.rearrange("p (a b) -> p a b", b=64), g3[:, :, 63:64].to_broadcast([4, 8, 64]), g3, ALU.subtract)
                P.act(REL, REL, AF.Exp)
        inproj(l, [2048, 2052], [4, 4], cons_bd)
        cols = sc[0:64, 17, 0:96].rearrange("p (c q h) -> p c q h", q=3, h=4)
        for c in range(8):
            for q, R in enumerate((RB, RBE, REL)):
                p_ = nps()
                P.tr(p_[0:64, 0:4], R[:, c * 64:(c + 1) * 64], ident[0:4, 0:4])
                P.copy(cols[:, c, q, :], p_[0:64, 0:4], eng="act")
        cwb = sc[:, 18:20, :].rearrange("p a b -> p (a b)")[:, 0:3 + TB]
        for h in range(4):
            qh = sc[:, 0, :]
            kh = sc[:, 1, :]
            vh = sc[:, 2, :]
            zh = sc[:, 3, :]
            kb = sc[:, 4, :]
            qg = sc[:, 5, :]
            tmp2 = sc[:, 6:8, :]
            rn = sc[:, 8, :]

            def cons(ci, p_, h=h):
                if ci < 3:
                    dst = (qh, kh, vh)[ci]
                    chunk = ci * 4 + h
                    conv4(l, "gconv", chunk, chunk, p_[:], dst, cwb)
                    P.act(dst, dst, AF.Silu)
                else:
                    P.act(zh, p_[:], AF.Silu)
            inproj(l, [h * 128, 512 + h * 128, 1024 + h * 128, 1536 + h * 128], [128] * 4, cons)
            rms_feat(cs_["ones_1"], [qh], RMS_EPS, tmp2, rn)
            P.stt(qh, qh, float(128 ** -0.5), rn, ALU.mult, ALU.mult)
            rms_feat(cs_["ones_1"], [kh], RMS_EPS, tmp2, rn)
            P.tt(kh, kh, rn, ALU.mult)
            pbe = nps()
            P.mm(pbe[:], cs_["sel"][:, h * 128:(h + 1) * 128], REG, start=True, stop=True)
            P.tt(qg, qh, pbe[:], ALU.mult)
            glc = sc[:, 9, 0:8]
            P.copy(glc, pbe[:].rearrange("p (c t) -> p c t", t=64)[:, :, 63], eng="act")
            pbb = nps()
            P.mm(pbb[:], cs_["sel"][:, h * 128:(h + 1) * 128], RB, start=True, stop=True)
            P.tt(kb, kh, pbb[:], ALU.mult)
            po = ps[7]
            S = gS[:, l, h, :]
            PA = sc[0:64, 6, :]
            TA = sc[0:64, 7, :]
            PB = sc[0:64, 10, :]
            TBb = sc[0:64, 11, :]
            QKT = sc[0:64, 8, :]
            Vm = sc[0:64, 18, :]
            selh = cs_["sel"][:, h * 128:h * 128 + 64]
            nselh = cs_["nsel"][:, h * 128:h * 128 + 64]

            def c3(a_):
                return a_.rearrange("p (c t) -> p c t", t=64)

            def cr(c):
                return slice(c * 64, (c + 1) * 64)
            pD = nps()
            for c in range(8):
                P.mm(pD[0:64, cr(c)], RG[:, cr(c)], selh, start=True, stop=False)
                P.mm(pD[0:64, cr(c)], nselh, RG[:, cr(c)], start=False, stop=False)
                P.mm(pD[0:64, cr(c)], ident[0:64, 0:64], cs_["m_strict_add"][:], start=False, stop=True)
            P.act(PB, pD[0:64, :], AF.Exp)
            pDT = nps()
            for c in range(8):
                P.mm(pDT[0:64, cr(c)], selh, RG[:, cr(c)], start=True, stop=False)
                P.mm(pDT[0:64, cr(c)], RG[:, cr(c)], nselh, start=False, stop=False)
                P.mm(pDT[0:64, cr(c)], ident[0:64, 0:64], cs_["m_inclT_add"][:], start=False, stop=True)
            P.act(TBb, pDT[0:64, :], AF.Exp)
            pA = nps()
            for c in range(8):
                P.mm(pA[0:64, cr(c)], kb[:, cr(c)], kh[:, cr(c)])
            P.tt(PA, pA[0:64, :], PB, ALU.mult)
            pAT = nps()
            for c in range(8):
                P.mm(pAT[0:64, cr(c)], kh[:, cr(c)], kb[:, cr(c)])
            P.tt(TA, pAT[0:64, :], TBb, ALU.mult)
            P.tt(c3(TA), c3(TA), cs_["m_strictT_01"][:].unsqueeze(1).to_broadcast([64, 8, 64]), ALU.mult)
            pQK = nps()
            for c in range(8):
                P.mm(pQK[0:64, cr(c)], kh[:, cr(c)], qh[:, cr(c)])
            P.tt(QKT, pQK[0:64, :], TBb, ALU.mult)
            P.tt(c3(Vm), ident[0:64, 0:64].unsqueeze(1).to_broadcast([64, 8, 64]), c3(TA), ALU.subtract)
            Pc, Tc, Pn2, Tn2 = PA, TA, PB, TBb
            for j in range(5):
                pq = nps()
                for c in range(8):
                    P.mm(pq[0:64, cr(c)], Tc[:, cr(c)], Pc[:, cr(c)])
                if j < 4:
                    pq2 = nps()
                    for c in range(8):
                        P.mm(pq2[0:64, cr(c)], Pc[:, cr(c)], Tc[:, cr(c)])
                P.copy(Pn2, pq[0:64, :], eng="act")
                if j < 4:
                    P.copy(Tn2, pq2[0:64, :])
                pv_ = nps()
                for c in range(8):
                    P.mm(pv_[0:64, cr(c)], Pn2[:, cr(c)], Vm[:, cr(c)])
                P.tt(Vm, Vm, pv_[0:64, :], ALU.add)
                Pc, Pn2 = Pn2, Pc
                Tc, Tn2 = Tn2, Tc

            def v8(lo):
                return sc[0:64, lo:lo + 2, :].rearrange("p a (c d) -> p (a c) d", d=128)
            rhs_w = v8(6)
            rhs_u = v8(10)
            kd = v8(19)
            uu = v8(21)
            wT = sc[:, 23, :]
            for g in range(2):
                pt = nps()
                for cc in range(4):
                    P.tr(pt[0:64, cc * 128:(cc + 1) * 128], kh[:, cr(4 * g + cc)], ident[:])
                ptv = pt[0:64, :].rearrange("p (c d) -> p c d", d=128)
                P.tt(rhs_w[:, 4 * g:4 * g + 4, :], ptv, cols[:, 4 * g:4 * g + 4, 1, h].unsqueeze(2).to_broadcast([64, 4, 128]), ALU.mult)
                P.tt(kd[:, 4 * g:4 * g + 4, :], ptv, cols[:, 4 * g:4 * g + 4, 2, h].unsqueeze(2).to_broadcast([64, 4, 128]), ALU.mult)
            for g in range(2):
                pt = nps()
                for cc in range(4):
                    P.tr(pt[0:64, cc * 128:(cc + 1) * 128], vh[:, cr(4 * g + cc)], ident[:])
                ptv = pt[0:64, :].rearrange("p (c d) -> p c d", d=128)
                P.tt(rhs_u[:, 4 * g:4 * g + 4, :], ptv, cols[:, 4 * g:4 * g + 4, 0, h].unsqueeze(2).to_broadcast([64, 4, 128]), ALU.mult)
            pw = nps()
            for c in range(8):
                P.mm(pw[:, cr(c)], rhs_w[:, c, :], Vm[:, cr(c)])
            P.copy(wT, pw[:], eng="act")
            for g in range(2):
                pu = nps()
                for cc in range(4):
                    c = 4 * g + cc
                    P.mm(pu[0:64, cc * 128:(cc + 1) * 128], Vm[:, cr(c)], rhs_u[:, c, :])
                P.copy(uu[:, 4 * g:4 * g + 4, :], pu[0:64, :].rearrange("p (c d) -> p c d", d=128), eng="act")
            for c in range(8):
                pws = nps()
                P.mm(pws[0:64, 0:128], wT[:, cr(c)], S)
                P.tt(uu[:, c, :], uu[:, c, :], pws[0:64, 0:128], ALU.subtract)
                P.mm(po[:, cr(c)], S, qg[:, cr(c)], start=True, stop=False)
                P.mm(po[:, cr(c)], uu[:, c, :], QKT[:, cr(c)], start=False, stop=True)
                pds = nps()
                P.mm(pds[:, 0:128], kd[:, c, :], uu[:, c, :])
                P.stt(S, S, glc[:, c:c + 1], pds[:, 0:128], ALU.mult, ALU.add)
            o = sc[:, 4, :]
            P.copy(o, po[:], eng="act")
            rms_feat(cs_["ones_h"], [o], RMS_EPS, tmp2, rn)
            P.tt(o, o, rn, ALU.mult)
            P.stt(ybuf[:, h, :], o, vcol("gnorm", l, 0), zh, ALU.mult, ALU.mult)

    def hgrn(l):
        dc = dcol[:, l, :]
        for h in range(4):
            q = sc[:, 0, :]
            fl = sc[:, 1, :]
            iv = sc[:, 2, :]
            gz = sc[:, 3, :]
            kk = sc[:, 4, :]
            b = sc[:, 5, :]
            qt = sc[:, 6, :]
            kt = sc[:, 7, :]
            qb = sc[:, 8, :]
            khh = sc[:, 9, :]
            tmp = sc[:, 10, :]
            ebl = sc[:, 11, 0:16]
            tmp2 = sc[:, 12:14, :]
            rn = sc[:, 14, :]

            def cons(ci, p_):
                if ci == 0:
                    P.copy(q, p_[:], eng="act")
                elif ci == 1:
                    P.act(fl, p_[:], AF.Sigmoid)
                elif ci == 2:
                    P.copy(iv, p_[:], eng="act")
                else:
                    P.act(gz, p_[:], AF.Silu)
            base = 2056
            inproj(l, [base + h * 128, base + 512 + h * 128, base + 1024 + h * 128, base + 1536 + h * 128], [128] * 4, cons)
            P.ts(fl, fl, dc[:, 76 + h:77 + h], dc[:, 72 + h:73 + h], ALU.mult, ALU.add)
            P.ts(kk, fl, -1.0, 1.0, ALU.mult, ALU.add)
            P.act(fl, fl, AF.Ln)
            P.scan(b, cs_["start32"][:], fl, 0.0)
            b3 = b.rearrange("p (a c) -> p a c", c=32)
            t3 = tmp.rearrange("p (a c) -> p a c", c=32)
            P.tt(t3, b3, b3[:, :, 15:16].to_broadcast([128, 16, 32]), ALU.subtract)
            P.act(qt, tmp, AF.Exp)
            P.act(kt, tmp, AF.Exp, scale=-1.0)
            P.stt(qt, q, float(128 ** -0.5), qt, ALU.mult, ALU.mult)
            P.tt(kt, kk, kt, ALU.mult)
            P.act(qb, b, AF.Exp)
            P.stt(qb, q, float(128 ** -0.5), qb, ALU.mult, ALU.mult)
            P.tt(t3, b3[:, :, 31:32].to_broadcast([128, 16, 32]), b3, ALU.subtract)
            P.act(khh, tmp, AF.Exp)
            P.tt(khh, kk, khh, ALU.mult)
            P.act(ebl, b3[:, :, 31], AF.Exp)
            po = ps[7]
            S = hS[:, l, h, :]
            attT_all = sc[0:32, 15, :]
            vT_all = sc[0:32, 16:20, :]
            khT_all = sc[0:32, 20:24, :]
            pa = nps()
            for c in range(16):
                cs = slice(c * 32, (c + 1) * 32)
                P.mm(pa[0:32, cs], kt[:, cs], qt[:, cs])
            P.stt(attT_all.rearrange("p (c t) -> p c t", t=32), pa[0:32, :].rearrange("p (c t) -> p c t", t=32), 1e30,
                  cs_["m_inclT_01"][0:32, 0:32].unsqueeze(1).to_broadcast([32, 16, 32]), ALU.min, ALU.mult)
            for g in range(4):
                pt = nps()
                for cc in range(4):
                    c = 4 * g + cc
                    P.tr(pt[0:32, cc * 128:(cc + 1) * 128], iv[:, c * 32:(c + 1) * 32], ident[:])
                P.copy(vT_all[:, g, :], pt[0:32, :], eng="act")
            for g in range(4):
                pt = nps()
                for cc in range(4):
                    c = 4 * g + cc
                    P.tr(pt[0:32, cc * 128:(cc + 1) * 128], khh[:, c * 32:(c + 1) * 32], ident[:])
                P.copy(khT_all[:, g, :], pt[0:32, :])
            for c in range(16):
                cs = slice(c * 32, (c + 1) * 32)
                attT = attT_all[:, cs]
                vT = vT_all[:, c // 4, (c % 4) * 128:(c % 4 + 1) * 128]
                khT = khT_all[:, c // 4, (c % 4) * 128:(c % 4 + 1) * 128]
                P.mm(po[:, cs], S, qb[:, cs], start=True, stop=False)
                P.mm(po[:, cs], vT, attT, start=False, stop=True)
                pds = nps()
                P.mm(pds[:, 0:128], khT, vT)
                P.stt(S, S, ebl[:, c:c + 1], pds[:, 0:128], ALU.mult, ALU.add)
            o = sc[:, 4, :]
            P.copy(o, po[:], eng="act")
            rms_feat(cs_["ones_h"], [o], RMS_EPS, tmp2, rn)
            P.tt(o, o, rn, ALU.mult)
            P.stt(ybuf[:, 4 + h, :], o, vcol("hnorm", l, 0), gz, ALU.mult, ALU.mult)

    def s5(l):
        dc = dcol[:, l, :]
        build_tables(l)
        u = sc[:, 0:4, :]

        def cons(ci, p_):
            P.copy(u[:, ci, :], p_[:], eng="act")
        inproj(l, [4104 + i * 128 for i in range(4)], [128] * 4, cons)
        gy = sc[:, 4:8, :]
        Bs = sc[:, 8:10, :].rearrange("p a b -> p (a b)").rearrange("p (r j m) -> p r j m", r=2, j=4)
        Cs = sc[:, 10:12, :].rearrange("p a b -> p (a b)").rearrange("p (r j m) -> p r j m", r=2, j=4)
        for i in range(4):
            P.dma("sp", Bs, Bp_d[l][:, :, 4 * i:4 * i + 4, :])
            P.dma("sp", Cs, Cp_d[l][:, :, 4 * i:4 * i + 4, :])
            py = ps[6]
            for jj in range(4):
                j = 4 * i + jj
                pr = nps()
                pi = nps()
                P.mm(pr[:], Bs[:, 0, jj, :], u[:, i, :])
                P.mm(pi[:], Bs[:, 1, jj, :], u[:, i, :])
                bre = sc[:, 12, :]
                bim = sc[:, 13, :]
                t1 = sc[:, 14, :]
                t2 = sc[:, 15, :]
                P.act(t1, pr[:], AF.Identity, scale=dc[:, 16 + j:17 + j])
                P.stt(bre, pi[:], dc[:, 48 + j:49 + j], t1, ALU.mult, ALU.add)
                P.act(t2, pi[:], AF.Identity, scale=dc[:, 16 + j:17 + j])
                P.stt(bim, pr[:], dc[:, 32 + j:33 + j], t2, ALU.mult, ALU.add)
                cb = tabc[:, j, :].unsqueeze(1).to_broadcast([128, 2, 256])
                sbb = tabs[:, j, :].unsqueeze(1).to_broadcast([128, 2, 256])

                def v3(a):
                    return a.rearrange("p (a b) -> p a b", b=256)
                wr = sc[:, 16, :]
                wi_ = sc[:, 17, :]
                sr = sc[:, 22, :]
                si = sc[:, 23, :]
                P.tt(v3(t1), v3(bre), cb, ALU.mult)
                P.tt(v3(t2), v3(bim), sbb, ALU.mult)
                P.tt(wr, t1, t2, ALU.add)
                P.tt(v3(t1), v3(bim), cb, ALU.mult)
                P.tt(v3(t2), v3(bre), sbb, ALU.mult)
                P.tt(wi_, t1, t2, ALU.subtract)
                xre = sc[:, 18 + 2 * (jj % 2), :]
                nxim = sc[:, 19 + 2 * (jj % 2), :]
                rb = dc[:, j:j + 1].to_broadcast([128, 256])
                for hf in range(2):
                    hs = slice(hf * 256, (hf + 1) * 256)
                    P.scan(sr[:, hs], rb, wr[:, hs], s5st[:, l, j, 0:1])
                    P.scan(si[:, hs], rb, wi_[:, hs], s5st[:, l, j, 1:2])
                    P.tt(t1[:, hs], sr[:, hs], tabc[:, j, :], ALU.mult)
                    P.tt(t2[:, hs], si[:, hs], tabs[:, j, :], ALU.mult)
                    P.tt(xre[:, hs], t1[:, hs], t2[:, hs], ALU.subtract)
                    P.tt(t1[:, hs], sr[:, hs], tabs[:, j, :], ALU.mult)
                    P.tt(t2[:, hs], si[:, hs], tabc[:, j, :], ALU.mult)
                    P.stt(nxim[:, hs], t1[:, hs], -1.0, t2[:, hs], ALU.mult, ALU.subtract)
                    P.copy(s5st[:, l, j, 0:1], xre[:, hf * 256 + 255:hf * 256 + 256], eng="act")
                    P.act(s5st[:, l, j, 1:2], nxim[:, hf * 256 + 255:hf * 256 + 256], AF.Copy, scale=-1.0)
                P.mm(py[:], Cs[:, 0, jj, :], xre, start=(jj == 0), stop=False)
                P.mm(py[:], Cs[:, 1, jj, :], nxim, start=False, stop=(jj == 3))
            yv = sc[:, 12, :]
            P.stt(yv, u[:, i, :], vcol("s5D", l, i), py[:], ALU.mult, ALU.add)
            P.act(gy[:, i, :], yv, AF.Gelu_apprx_tanh)
        gl = sc[:, 8:12, :]
        P.dma("sp", gl, glu_d[l])
        so = sc[:, 12:16, :]
        for oc in range(4):
            pg = nps()
            for k in range(4):
                P.mm(pg[:], gl[:, k, oc * 128:(oc + 1) * 128], gy[:, k, :], start=(k == 0), stop=(k == 3))
            t = sc[:, 16, :]
            P.act(t, pg[:], AF.Sigmoid, bias=vcol("glub", l, oc))
            P.tt(so[:, oc, :], gy[:, oc, :], t, ALU.mult)
        rn = sc[:, 17, :]
        rms_feat(cs_["ones_w"], [so[:, i, :] for i in range(4)], RMS_EPS, sc[:, 18:20, :], rn)
        for i in range(4):
            P.stt(ybuf[:, 8 + i, :], so[:, i, :], vcol("bng", l, i), rn, ALU.mult, ALU.mult)

    def lru(l):
        dc = dcol[:, l, :]
        Wb = sc[:, 8:10, :].rearrange("p a b -> p (a b)").rearrange("p (r j m) -> p r j m", r=2, j=4)
        P.dma("sp", Wb, Wb_d[l])
        xc = sc[:, 0:4, :]
        gg = sc[:, 4:8, :]
        cwb = sc[:, 10:12, :].rearrange("p a b -> p (a b)")[:, 0:3 + TB]

        def cons(ci, p_):
            if ci < 4:
                conv4(l, "lconv", ci, 12 + ci, p_[:], xc[:, ci, :], cwb, bias=vcol("lconvb", l, ci))
            else:
                P.act(gg[:, ci - 4, :], p_[:], AF.Gelu_apprx_tanh)
        inproj(l, [4616 + i * 128 for i in range(8)], [128] * 8, cons)
        yo = sc[:, 12:16, :]
        for i in range(4):
            pr = nps()
            pi = nps()
            P.mm(pr[:], Wb[:, 0, i, :], xc[:, i, :])
            P.mm(pi[:], Wb[:, 1, i, :], xc[:, i, :])
            r = sc[:, 16, :]
            gi = sc[:, 17, :]
            a = sc[:, 18, :]
            a2 = sc[:, 19, :]
            P.act(r, pr[:], AF.Sigmoid, bias=vcol("lba", l, i))
            P.act(gi, pi[:], AF.Sigmoid, bias=vcol("lbx", l, i))
            P.act(a, r, AF.Exp, scale=dc[:, 64 + i:65 + i])
            P.act(a2, r, AF.Exp, scale=dc[:, 68 + i:69 + i])
            P.ts(a2, a2, -1.0, 1.0, ALU.mult, ALU.add)
            P.act(a2, a2, AF.Sqrt)
            P.tt(gi, gi, xc[:, i, :], ALU.mult)
            P.tt(gi, gi, a2, ALU.mult)
            hh = sc[:, 20, :]
            P.scan(hh, a, gi, lrust[:, l, i:i + 1])
            P.copy(lrust[:, l, i:i + 1], hh[:, TB - 1:TB], eng="act")
            P.tt(yo[:, i, :], hh, gg[:, i, :], ALU.mult)
        rn = sc[:, 16, :]
        rms_feat(cs_["ones_w"], [yo[:, i, :] for i in range(4)], RMS_EPS, sc[:, 18:20, :], rn)
        for i in range(4):
            P.stt(ybuf[:, 12 + i, :], yo[:, i, :], vcol("bng", l, 4 + i), rn, ALU.mult, ALU.mult)

    def mixer(l):
        if 'gdn' in ST:
            gdn(l)
        if 'hgrn' in ST:
            hgrn(l)
        if 's5' in ST:
            s5(l)
        if 'lru' in ST:
            lru(l)
        if 'oproj' not in ST:
            return
        wv = wview(w_out[l])
        for og in range(4):
            sl = wload([(wv[:, :, og * 512:(og + 1) * 512], 0, 0)])
            for oc in range(4):
                p_ = nps()
                for k in range(NCH):
                    P.mm(p_[:], sl[:, k, oc * 128:(oc + 1) * 128], ybuf[:, k, :], start=(k == 0), stop=(k == NCH - 1))
                resid(og * 4 + oc, p_[:], 1.0)
        layer_norm(l, 1)

    xv = xT.rearrange("(c p) t -> p c t", p=128)
    ov = oT.rearrange("(c p) t -> p c t", p=128)
    if pipe:
        GROUPS = [[b_, b_ + int(pipe)] for b_ in range(int(pipe))]
    for blk in range(NST):
        P.dma("sp", x[:], xv[:, :, blk * TB:(blk + 1) * TB])
        if pipe:
            if blk > 0:
                xr = sc[:, 0:16, :]
                for j in range(2):
                    P.dma("sp", xr[:, 8 * j:8 * j + 8, :], cc_recv[j].ap()[0:1024, :].rearrange("(c p) t -> p c t", p=128))
            for c in range(NCH if blk > 0 else 0):
                P.ts(x[:, c, :], x[:, c, :], cmask[:, 0:1], None, ALU.mult)
                P.stt(x[:, c, :], xr[:, c, :], cmask[:, 1:2], x[:, c, :], ALU.mult, ALU.add)
        for c in range(NCH):
            P.copy(xb[:, c, :], x[:, c, :], eng="act" if c % 2 else "dve")
        for l in range(depth):
            if 'ffn1' in ST:
                ffn(l, 0, 0)
            mixer(l)
            if 'ffn2' in ST:
                ffn(l, 1, 2)
            if 'ple' in ST:
                ple(l, blk)
        P.dma("sp", ov[:, :, blk * TB:(blk + 1) * TB], x[:])
        if pipe and blk < NST - 1:
            for j in range(2):
                P.dma("sp", cc_send[j].ap().rearrange("(c p) t -> p c t", p=128), x[:, 8 * j:8 * j + 8, :])
                P.cc(GROUPS, cc_send[j].ap(), cc_recv[j].ap())
        if pipe and blk == 0:
            for t_ in (gS, hS, s5st, lrust, ctail):
                P.ts(t_[:], t_[:], cmask[:, 0:1], None, ALU.mult)
    P.wait_all_dma("sp")
    P.emit(st)
    st.close()
    return nc, P


_CACHE = {}


def prep_inputs(inp, depth, extra_vec=None):
    vp = build_vec(inp, depth)
    for k_, v_ in (extra_vec or {}).items():
        vp.add(k_, v_)
    vecarr = vp.build()
    mats = [build_mats(inp, l) for l in range(depth)]
    shared = {
        "ffn_wi": np.ascontiguousarray(inp["ffn_wi"][:depth], dtype=np.float32),
        "ffn_wo": np.ascontiguousarray(inp["ffn_wo"][:depth], dtype=np.float32),
        "mix_w_in": np.ascontiguousarray(inp["mix_w_in"][:depth], dtype=np.float32),
        "mix_w_out": np.ascontiguousarray(inp["mix_w_out"][:depth], dtype=np.float32),
        "ple_w": np.ascontiguousarray(inp["ple_w"][:depth], dtype=np.float32),
        "ple_gate_w": np.ascontiguousarray(inp["ple_gate_w"][:depth], dtype=np.float32),
        "vec": vecarr,
        "Bp": np.stack([m[0] for m in mats]),
        "Cp": np.stack([m[1] for m in mats]),
        "glu": np.stack([m[2] for m in mats]),
        "Wb": np.stack([m[3] for m in mats]),
    }
    for k, v in build_consts().items():
        shared["c_" + k] = v
    return shared, vp.index, vecarr.shape[1]


def run_model(inp, depth=2, n_cores=8, dbg=None, stages=None):
    inp = {k: np.asarray(v) for k, v in inp.items()}
    x = inp["x"]
    p = inp["p"]
    B, T, _ = x.shape
    shared, vidx, nvec = prep_inputs(inp, depth)
    key = (T, depth, nvec, dbg, None if stages is None else tuple(sorted(stages)))
    if key not in _CACHE:
        _CACHE[key] = build_program(T, depth, vidx, nvec, dbg, stages)
    nc, P = _CACHE[key]
    in_maps = []
    for c in range(n_cores):
        b = c % B
        m = dict(shared)
        m["xT"] = np.ascontiguousarray(x[b].T)
        m["pT"] = np.ascontiguousarray(p[:depth, b].transpose(0, 2, 1))
        m["cmask"] = np.ones((128, 4), np.float32)
        in_maps.append(m)
    res = run_bass_kernel_spmd(nc, in_maps, core_ids=list(range(n_cores)))
    out = np.stack([np.ascontiguousarray(res.results[b]["oT"].T) for b in range(B)])
    if dbg is not None:
        return out, res.results[0]["dbg"]
    return out


LAYER_KEYS = ("ln_g", "ln_b", "ffn_wi", "ffn_wo", "mix_w_in", "mix_w_out", "gdn_conv_w", "gdn_A_log", "gdn_dt_bias",
              "gdn_norm_g", "hgrn_lb_logits", "hgrn_norm_g", "s5_lam_re", "s5_lam_im", "s5_log_dt", "s5_B_re", "s5_B_im",
              "s5_C_re", "s5_C_im", "s5_D", "s5_glu_w", "s5_glu_b", "lru_conv_w", "lru_conv_b", "lru_wa", "lru_ba",
              "lru_wx", "lru_bx", "lru_param", "branch_norm_g", "ple_w", "ple_gate_w")


def pipe_maps(inp, n_seq, sim=False):
    x = inp["x"]
    p = inp["p"]
    B, T, _ = x.shape
    NB = T // TB
    TT = (NB + 1) * TB
    per_layer = []
    vidx = nvec = None
    for l in range(2):
        il = {k: inp[k][l:l + 1] for k in LAYER_KEYS}
        shared, vidx_l, nvec_l = prep_inputs(il, 1, extra_vec={"hlbA": chunkcols(inp["hgrn_lb_logits"][0]),
                                                                "hlbB": chunkcols(inp["hgrn_lb_logits"][1])})
        per_layer.append(shared)
        vidx, nvec = vidx_l, nvec_l
    maps = {}
    for l in range(2):
        for b in range(n_seq):
            m = dict(per_layer[l])
            xt = np.zeros((D, TT), np.float32)
            pt = np.zeros((1, 256, TT), np.float32)
            cm = np.zeros((128, 4), np.float32)
            if l == 0:
                xt[:, :T] = x[b].T
                xt[:, T:] = x[b, T - TB:].T
                pt[0, :, :T] = p[0, b].T
                cm[:, 0] = 1.0
            else:
                xt[:, :TB] = x[b, :TB].T
                pt[0, :, TB:] = p[1, b].T
                cm[:, 1] = 1.0
            m["xT"] = xt
            m["pT"] = pt
            m["cmask"] = cm
            maps[(l, b)] = m
    return maps, vidx, nvec


def run_model_pipe(inp):
    inp = {k: np.asarray(v) for k, v in inp.items()}
    B, T, _ = inp["x"].shape
    maps, vidx, nvec = pipe_maps(inp, B)
    key = ("pipe", T, nvec)
    if key not in _CACHE:
        _CACHE[key] = build_program(T, 1, vidx, nvec, None, None, pipe=4)
    nc, P = _CACHE[key]
    in_maps = [maps[(c // 4, c % 4)] for c in range(8)]
    res = run_bass_kernel_spmd(nc, in_maps, core_ids=list(range(8)))
    return np.stack([np.ascontiguousarray(res.results[4 + b]["oT"][:, TB:].T) for b in range(B)])


def kernel(**inputs):
    return run_model_pipe(inputs).astype(np.float32)
```

```python
from contextlib import ExitStack
import numpy as np
import concourse.bass as bass
import concourse.mybir as mybir
from concourse.bass_utils import run_bass_kernel_spmd

F32 = mybir.dt.float32
BF16 = mybir.dt.bfloat16
AF = mybir.ActivationFunctionType
ALU = mybir.AluOpType
DTSIZE = {F32: 4, BF16: 2}

ENGS = ("pe", "act", "dve", "pool", "sp")
SEM_LIMIT = 30000
N_DMA_SEMS = 24

D = 2048
DFF = 5632
W_ = 512
INC = 5640
TB = 512
NCH = 16
ALPHA = 4 ** 0.25
LN_EPS = 1e-5
RMS_EPS = 1e-6


def _region(ap):
    sp = str(ap.space)
    if "SB" not in sp and "PSUM" not in sp:
        if ap.name.startswith("cc_"):
            return (ap.name, 0, 1, 0, 1)
        return None
    pairs = ap.ap
    sz = DTSIZE[ap.dtype]
    pstep, pcount = pairs[0]
    off = ap.offset
    if pstep > 0:
        p0 = off // pstep
        f0 = off % pstep
    else:
        p0 = 0
        f0 = off
    ext = 1
    for s, c in pairs[1:]:
        if s > 0:
            ext += (c - 1) * s
    if "PSUM" in sp:
        return (ap.name, 0, 128, 0, 2048)
    return (ap.name, p0, p0 + pcount, f0 * sz, (f0 + ext) * sz)


class Prog:
    def __init__(self, nc):
        self.nc = nc
        self.streams = {e: [] for e in ENGS}
        self.cnt = {e: 0 for e in ENGS}
        self.epoch = {e: 0 for e in ENGS}
        self.synced = {e: {} for e in ENGS}
        self.hist = {}
        self.dma_uses = [0] * N_DMA_SEMS
        self.dma_rr = 0
        self.dma_rrq = [0, 0]
        self.cc_uses = 0
        self.sem_keys = set()
        self.nops = 0

    def _deps(self, eng, reads, writes):
        waits = {}
        syn = self.synced[eng]
        regs = []
        for ap in reads:
            r = _region(ap)
            if r is not None:
                regs.append((r, r[0].startswith("ps")))
        for ap in writes:
            r = _region(ap)
            if r is not None:
                regs.append((r, True))
        for (name, p0, p1, b0, b1), isw in regs:
            lst = self.hist.get(name)
            if not lst:
                continue
            for rec in lst:
                key, val, rw, q0, q1, c0, c1 = rec
                if not (isw or rw):
                    continue
                if q1 <= p0 or p1 <= q0 or c1 <= b0 or b1 <= c0:
                    continue
                if key[0] == "pe" and eng == "pe":
                    continue
                if syn.get(key, 0) >= val:
                    continue
                if waits.get(key, 0) < val:
                    waits[key] = val
        for k, v in waits.items():
            syn[k] = v
        return list(waits.items()), regs

    def _record(self, regs, key, val):
        for (name, p0, p1, b0, b1), isw in regs:
            lst = self.hist.setdefault(name, [])
            if isw:
                lst[:] = [r for r in lst if not (r[3] >= p0 and r[4] <= p1 and r[5] >= b0 and r[6] <= b1)]
            else:
                lst[:] = [r for r in lst if not ((not r[2]) and r[0] == key and r[3] >= p0 and r[4] <= p1
                                                 and r[5] >= b0 and r[6] <= b1)]
            lst.append((key, val, isw, p0, p1, b0, b1))

    def op(self, eng, fn, reads=(), writes=()):
        waits, regs = self._deps(eng, reads, writes)
        if self.cnt[eng] >= SEM_LIMIT:
            self.epoch[eng] += 1
            self.cnt[eng] = 0
        self.cnt[eng] += 1
        key = (eng, self.epoch[eng])
        val = self.cnt[eng]
        self.sem_keys.add(key)
        self.streams[eng].append((waits, fn, (key, 1)))
        self._record(regs, key, val)
        if eng == "pe":
            self.synced[eng][key] = val
        self.nops += 1

    def dma(self, queue, out, in_, **kw):
        waits, regs = self._deps(queue, [in_], [out])
        half = N_DMA_SEMS // 2
        qi = 0 if queue == "sp" else 1
        j = qi * half + self.dma_rrq[qi]
        self.dma_rrq[qi] = (self.dma_rrq[qi] + 1) % half
        key = ("dma", j)
        prev = self.dma_uses[j] * 16
        if prev and self.synced[queue].get(key, 0) < prev:
            waits.append((key, prev))
            self.synced[queue][key] = prev
        self.dma_uses[j] += 1
        val = self.dma_uses[j] * 16
        self.sem_keys.add(key)
        self.streams[queue].append((waits, lambda e: e.dma_start(out=out, in_=in_, **kw), (key, 16)))
        self._record(regs, key, val)
        self.nops += 1

    def cc(self, groups, in_ap, out_ap):
        waits, regs = self._deps("pool", [in_ap], [out_ap])
        self.cc_uses += 1
        key = ("cc", 0)
        self.sem_keys.add(key)
        self.streams["pool"].append((waits, lambda e: e.collective_compute(
            "AllGather", ALU.bypass, replica_groups=groups, ins=[in_ap], outs=[out_ap]), (key, 1)))
        self._record(regs, key, self.cc_uses)
        self.nops += 1

    def wait_all_dma(self, queue="sp"):
        waits = []
        for j in range(N_DMA_SEMS):
            if self.dma_uses[j]:
                waits.append((("dma", j), self.dma_uses[j] * 16))
        self.streams[queue].append((waits, None, None))

    def mm(self, out, lhsT, rhs, start=True, stop=True):
        self.op("pe", lambda e: e.matmul(out, lhsT, rhs, start=start, stop=stop), [lhsT, rhs], [out])

    def tr(self, out, in_, ident):
        self.op("pe", lambda e: e.transpose(out, in_, ident), [in_, ident], [out])

    def act(self, out, in_, func, bias=None, scale=1.0):
        rd = [in_]
        kw = {}
        if bias is not None:
            kw["bias"] = bias
            if not isinstance(bias, (int, float)):
                rd.append(bias)
        if not isinstance(scale, (int, float)):
            rd.append(scale)
        self.op("act", lambda e: e.activation(out, in_, func, scale=scale, **kw), rd, [out])

    def tt(self, out, a, b, op, eng="dve"):
        self.op(eng, lambda e: e.tensor_tensor(out, a, b, op), [a, b], [out])

    def ts(self, out, a, s1, s2, op0, op1=None, eng="dve"):
        rd = [a]
        for s in (s1, s2):
            if s is not None and not isinstance(s, (int, float)):
                rd.append(s)
        if op1 is None:
            self.op(eng, lambda e: e.tensor_scalar(out, a, s1, None, op0), rd, [out])
        else:
            self.op(eng, lambda e: e.tensor_scalar(out, a, s1, s2, op0, op1), rd, [out])

    def stt(self, out, a, s, b, op0, op1):
        rd = [a, b]
        if not isinstance(s, (int, float)):
            rd.append(s)
        self.op("dve", lambda e: e.scalar_tensor_tensor(out, a, s, b, op0, op1), rd, [out])

    def copy(self, out, in_, eng="dve"):
        if eng == "act":
            self.op("act", lambda e: e.copy(out, in_), [in_], [out])
        else:
            self.op(eng, lambda e: e.tensor_copy(out, in_), [in_], [out])

    def memset(self, out, v, eng="dve"):
        self.op(eng, lambda e: e.memset(out, v), [], [out])

    def recip(self, out, in_):
        self.op("dve", lambda e: e.reciprocal(out, in_), [in_], [out])

    def scan(self, out, d0, d1, init):
        rd = [d0, d1]
        if not isinstance(init, (int, float)):
            rd.append(init)
        self.op("dve", lambda e: e.tensor_tensor_scan(out, d0, d1, init, ALU.mult, ALU.add), rd, [out])

    def emit(self, stack):
        nc = self.nc
        sems = {}
        for key in sorted(self.sem_keys, key=str):
            sems[key] = stack.enter_context(nc.semaphore("s_%s_%d" % key))
        block = stack.enter_context(nc.Block())

        def run(stream):
            def body(eng):
                for waits, fn, inc in stream:
                    for k, v in waits:
                        eng.wait_ge(sems[k], v)
                    if fn is None:
                        continue
                    ins = fn(eng)
                    if inc is not None:
                        ins.then_inc(sems[inc[0]], inc[1])
            return body

        block.tensor(run(self.streams["pe"]))
        block.scalar(run(self.streams["act"]))
        block.vector(run(self.streams["dve"]))
        block.gpsimd(run(self.streams["pool"]))
        block.sync(run(self.streams["sp"]))


class VecPack:
    def __init__(self):
        self.cols = []
        self.index = {}

    def add(self, name, arr2d):
        a = np.zeros((128, arr2d.shape[1]), np.float32)
        a[: arr2d.shape[0]] = arr2d
        self.index[name] = sum(c.shape[1] for c in self.cols)
        self.cols.append(a)

    def build(self):
        return np.ascontiguousarray(np.concatenate(self.cols, axis=1))


def chunkcols(v):
    return np.ascontiguousarray(v.reshape(-1, 128).T)


def build_vec(inp, depth):
    vp = VecPack()
    for l in range(depth):
        vp.add("ln_g%d" % l, np.concatenate([chunkcols(inp["ln_g"][l, s]) for s in range(4)], axis=1))
        vp.add("ln_b%d" % l, np.concatenate([chunkcols(inp["ln_b"][l, s]) for s in range(4)], axis=1))
        cw = inp["gdn_conv_w"][l]
        vp.add("gconv%d" % l, np.stack([chunkcols(cw[j]) for j in range(4)], axis=2).reshape(128, 48))
        vp.add("gnorm%d" % l, inp["gdn_norm_g"][l].reshape(128, 1))
        vp.add("hnorm%d" % l, inp["hgrn_norm_g"][l].reshape(128, 1))
        vp.add("hlb%d" % l, chunkcols(inp["hgrn_lb_logits"][l]))
        vp.add("lam_re%d" % l, chunkcols(inp["s5_lam_re"][l].reshape(-1)))
        vp.add("lam_im%d" % l, chunkcols(inp["s5_lam_im"][l].reshape(-1)))
        vp.add("logdt%d" % l, chunkcols(np.repeat(inp["s5_log_dt"][l], 64)))
        vp.add("s5D%d" % l, chunkcols(inp["s5_D"][l]))
        vp.add("glub%d" % l, chunkcols(inp["s5_glu_b"][l]))
        lw = inp["lru_conv_w"][l]
        vp.add("lconv%d" % l, np.stack([chunkcols(lw[j]) for j in range(4)], axis=2).reshape(128, 16))
        vp.add("lconvb%d" % l, chunkcols(inp["lru_conv_b"][l]))
        vp.add("lba%d" % l, chunkcols(inp["lru_ba"][l]))
        vp.add("lbx%d" % l, chunkcols(inp["lru_bx"][l]))
        vp.add("lparam%d" % l, chunkcols(inp["lru_param"][l]))
        vp.add("bng%d" % l, np.concatenate([chunkcols(inp["branch_norm_g"][l, s]) for s in range(2)], axis=1))
        vp.add("alog%d" % l, inp["gdn_A_log"][l].reshape(4, 1))
        vp.add("dtb%d" % l, inp["gdn_dt_bias"][l].reshape(4, 1))
    return vp


def build_consts():
    c = {}
    c["ident"] = np.eye(128, dtype=np.float32)
    c["ones_d"] = np.full((128, 128), 1.0 / D, np.float32)
    c["ones_w"] = np.full((128, 128), 1.0 / W_, np.float32)
    c["ones_1"] = np.full((128, 128), 1.0, np.float32)
    c["ones_h"] = np.full((128, 128), 1.0 / 128, np.float32)
    i = np.arange(64)
    NEG = -1e30
    c["m_strict_add"] = np.where(i[:, None] > i[None, :], 0.0, NEG).astype(np.float32)
    c["m_inclT_add"] = np.where(i[:, None] <= i[None, :], 0.0, NEG).astype(np.float32)
    c["m_strictT_01"] = (i[:, None] < i[None, :]).astype(np.float32)
    c["m_inclT_01"] = (i[:, None] <= i[None, :]).astype(np.float32)
    m64 = np.ones((128, TB), np.float32)
    m64[:, ::64] = 0.0
    c["start64"] = m64
    m32 = np.ones((128, TB), np.float32)
    m32[:, ::32] = 0.0
    c["start32"] = m32
    sel = np.zeros((4, 4, 128), np.float32)
    for h in range(4):
        sel[h, h, :] = 1.0
    c["sel"] = sel.reshape(4, 512)
    c["nsel"] = np.where(sel != 0, -1.0, 0.0).astype(np.float32).reshape(4, 512)
    return c


def build_mats(inp, l):
    Bre, Bim = inp["s5_B_re"][l], inp["s5_B_im"][l]
    Cre, Cim = inp["s5_C_re"][l], inp["s5_C_im"][l]
    Bp = np.zeros((128, 2, 16, 128), np.float32)
    Cp = np.zeros((128, 2, 16, 128), np.float32)
    for j in range(16):
        for gi in range(2):
            g = 2 * j + gi
            lg = g % 8
            Bp[16 * lg:16 * lg + 16, 0, j, 64 * gi:64 * gi + 64] = Bre[g].T
            Bp[16 * lg:16 * lg + 16, 1, j, 64 * gi:64 * gi + 64] = Bim[g].T
            Cp[64 * gi:64 * gi + 64, 0, j, 16 * lg:16 * lg + 16] = Cre[g].T
            Cp[64 * gi:64 * gi + 64, 1, j, 16 * lg:16 * lg + 16] = Cim[g].T
    glu = np.ascontiguousarray(inp["s5_glu_w"][l].reshape(4, 128, 512).transpose(1, 0, 2))
    Wb = np.zeros((128, 2, 4, 128), np.float32)
    for i in range(4):
        for bi in range(2):
            n = 2 * i + bi
            Wb[64 * bi:64 * bi + 64, 0, i, 64 * bi:64 * bi + 64] = inp["lru_wa"][l, n]
            Wb[64 * bi:64 * bi + 64, 1, i, 64 * bi:64 * bi + 64] = inp["lru_wx"][l, n]
    return Bp, Cp, glu, Wb


def build_program(T, depth, vidx, nvec, dbg=None, stages=None, pipe=False):
    ST = stages if stages is not None else {'derived', 'ffn1', 'gdn', 'hgrn', 's5', 'lru', 'oproj', 'ffn2', 'ple'}
    NB = T // TB
    NST = NB + 1 if pipe else NB
    TT = NST * TB
    nc = bass.Bass("TRN2", target_bir_lowering=False)

    def din(name, shape):
        return nc.dram_tensor(name, list(shape), F32, kind="ExternalInput").ap()

    xT = din("xT", [D, TT])
    pT = din("pT", [depth, 256, TT])
    cmask_d = din("cmask", [128, 4])
    wi = din("ffn_wi", [depth, 2, D, 2 * DFF])
    wo = din("ffn_wo", [depth, 2, DFF, D])
    w_in = din("mix_w_in", [depth, D, INC])
    w_out = din("mix_w_out", [depth, D, D])
    ple_w = din("ple_w", [depth, 256, D])
    ple_gw = din("ple_gate_w", [depth, D, D])
    vec_d = din("vec", [128, nvec])
    cst = {k: din("c_" + k, v.shape) for k, v in build_consts().items()}
    Bp_d = din("Bp", [depth, 128, 2, 16, 128])
    Cp_d = din("Cp", [depth, 128, 2, 16, 128])
    glu_d = din("glu", [depth, 128, 4, 512])
    Wb_d = din("Wb", [depth, 128, 2, 4, 128])
    oT = nc.dram_tensor("oT", [D, TT], F32, kind="ExternalOutput").ap()
    cc_send = [nc.dram_tensor("cc_send%d" % j, [512, TB], F32) for j in range(4)] if pipe else None
    cc_recv = [nc.dram_tensor("cc_recv%d" % j, [1024, TB], F32) for j in range(4)] if pipe else None
    dbg_d = None
    if dbg is not None:
        dbg_d = nc.dram_tensor("dbg", [128, dbg, TB], F32, kind="ExternalOutput").ap()

    st = ExitStack()

    def sb(name, shape, dt=F32):
        return st.enter_context(nc.sbuf_tensor("s_" + name, list(shape), dt))

    P = Prog(nc)
    x = sb("x", [128, NCH, TB])
    xb = sb("xb", [128, NCH, TB], BF16)
    NSC = 24
    sc = sb("sc", [128, NSC, TB])
    NWR = 2
    wring = [sb("wr%d" % i, [128, 16, 512], BF16) for i in range(NWR)]
    vec = sb("vec", [128, nvec])
    cmask = sb("cmask", [128, 4])
    cs_ = {k: sb("k_" + k, v.shape) for k, v in build_consts().items()}
    ybuf = sb("ybuf", [128, NCH, TB], BF16)
    tabc = sb("tabc", [128, 16, 256])
    tabs = sb("tabs", [128, 16, 256])
    NDC = 96
    dcol = sb("dcol", [128, depth, NDC])
    gS = sb("gS", [128, depth, 4, 128])
    hS = sb("hS", [128, depth, 4, 128])
    s5st = sb("s5st", [128, depth, 16, 2])
    lrust = sb("lrust", [128, depth, 4])
    ctail = sb("ctail", [128, depth, 16, 3])
    pb = sb("pb", [128, 2, TB], BF16)
    ps = [st.enter_context(nc.psum_tensor("ps%d" % i, [128, TB], F32)) for i in range(8)]
    psc = [0]

    def nps():
        psc[0] = (psc[0] + 1) % 6
        return ps[psc[0]]

    ident = cs_["ident"]
    wslot = [0]

    def V(name, l=None):
        return vidx[name if l is None else "%s%d" % (name, l)]

    def vcol(name, l, c):
        o = V(name, l) + c
        return vec[:, o:o + 1]

    P.dma("sp", vec[:], vec_d)
    P.dma("sp", cmask[:], cmask_d)
    for k in cst:
        P.dma("sp", cs_[k][:], cst[k])
    for t_ in (gS, hS, s5st, lrust, ctail):
        P.memset(t_[:], 0.0)

    MAGIC = 12582912.0

    def sincos(ang, out_sin, out_cos, t_a, t_b):
        for shift, dst in ((0.0, out_sin), (float(0.5 * np.pi), out_cos)):
            P.ts(t_a, ang, shift, float(1.0 / (2 * np.pi)), ALU.add, ALU.mult)
            P.ts(t_b, t_a, MAGIC, None, ALU.add)
            P.ts(t_b, t_b, -MAGIC, None, ALU.add)
            P.tt(t_a, t_a, t_b, ALU.subtract)
            P.act(dst, t_a, AF.Sin, scale=float(2 * np.pi))

    tmpc = sc[:, 0, :]
    for l in range(depth if 'derived' in ST else 0):
        dc = dcol[:, l, :]
        lre = vec[:, V("lam_re", l):V("lam_re", l) + 16]
        lim = vec[:, V("lam_im", l):V("lam_im", l) + 16]
        ldt = vec[:, V("logdt", l):V("logdt", l) + 16]
        dt = tmpc[:, 0:16]
        P.act(dt, ldt, AF.Exp)
        ex = tmpc[:, 16:32]
        P.tt(ex, lre, dt, ALU.mult)
        ang = tmpc[:, 32:48]
        P.tt(ang, lim, dt, ALU.mult)
        P.act(dc[:, 0:16], ex, AF.Exp)
        angr = tmpc[:, 48:64]
        sincos(ang, tmpc[:, 64:80], tmpc[:, 96:112], angr, tmpc[:, 80:96])
        sn = tmpc[:, 64:80]
        cs = tmpc[:, 96:112]
        abre = tmpc[:, 112:128]
        abim = tmpc[:, 128:144]
        P.tt(abre, dc[:, 0:16], cs, ALU.mult)
        P.tt(abim, dc[:, 0:16], sn, ALU.mult)
        den = tmpc[:, 144:160]
        t1 = tmpc[:, 160:176]
        P.tt(den, lre, lre, ALU.mult)
        P.tt(t1, lim, lim, ALU.mult)
        P.tt(den, den, t1, ALU.add)
        rden = tmpc[:, 176:192]
        P.recip(rden, den)
        nr = tmpc[:, 192:208]
        P.ts(nr, abre, -1.0, None, ALU.add)
        t2 = tmpc[:, 208:224]
        P.tt(t1, nr, lre, ALU.mult)
        P.tt(t2, abim, lim, ALU.mult)
        P.tt(t1, t1, t2, ALU.add)
        P.tt(dc[:, 16:32], t1, rden, ALU.mult)
        P.tt(t1, abim, lre, ALU.mult)
        P.tt(t2, nr, lim, ALU.mult)
        P.tt(t1, t1, t2, ALU.subtract)
        P.tt(dc[:, 32:48], t1, rden, ALU.mult)
        P.ts(dc[:, 48:64], dc[:, 32:48], -1.0, None, ALU.mult)
        lp = vec[:, V("lparam", l):V("lparam", l) + 4]
        t3 = tmpc[:, 224:228]
        P.act(t3, lp, AF.Exp, scale=-1.0)
        P.act(t3, t3, AF.Ln, bias=1.0)
        P.ts(dc[:, 64:68], t3, -8.0, None, ALU.mult)
        P.ts(dc[:, 68:72], t3, -16.0, None, ALU.mult)
        if pipe:
            l0 = vec[:, V("hlbA"):V("hlbA") + 4]
            l1 = vec[:, V("hlbB"):V("hlbB") + 4]
            t4 = tmpc[:, 228:232]
            P.tt(t4, l1, l0, ALU.subtract)
            P.act(t4, t4, AF.Sigmoid)
            P.ts(dc[:, 72:76], t4, cmask[:, 1:2], None, ALU.mult)
        elif l == 0:
            P.memset(dc[:, 72:76], 0.0)
        else:
            l0 = vec[:, V("hlb", 0):V("hlb", 0) + 4]
            l1 = vec[:, V("hlb", 1):V("hlb", 1) + 4]
            t4 = tmpc[:, 228:232]
            P.tt(t4, l1, l0, ALU.subtract)
            P.act(dc[:, 72:76], t4, AF.Sigmoid)
        P.ts(dc[:, 76:80], dc[:, 72:76], -1.0, 1.0, ALU.mult, ALU.add)
        al = vec[0:4, V("alog", l):V("alog", l) + 1]
        P.act(dc[0:4, 80:81], al, AF.Exp)
        P.ts(dc[0:4, 80:81], dc[0:4, 80:81], -1.0, None, ALU.mult)
    seeds = sb("seeds", [128, depth, 2, 16])
    for l in range(depth if 'derived' in ST else 0):
        lim = vec[:, V("lam_im", l):V("lam_im", l) + 16]
        ldt = vec[:, V("logdt", l):V("logdt", l) + 16]
        dt = tmpc[:, 0:16]
        P.act(dt, ldt, AF.Exp)
        ang = tmpc[:, 32:48]
        P.tt(ang, lim, dt, ALU.mult)
        sincos(ang, seeds[:, l, 1, :], seeds[:, l, 0, :], tmpc[:, 48:64], tmpc[:, 80:96])

    cur_tab_layer = [None]

    def build_tables(l):
        if cur_tab_layer[0] == l:
            return
        cur_tab_layer[0] = l
        P.copy(tabc[:, :, 0:1], seeds[:, l, 0, :].unsqueeze(2))
        P.copy(tabs[:, :, 0:1], seeds[:, l, 1, :].unsqueeze(2))
        k = 1
        t1 = sc[:, 0:4, :].rearrange("p a b -> p (a b)")[:, 0:16 * 128].rearrange("p (a b) -> p a b", b=128)
        t2 = sc[:, 4:8, :].rearrange("p a b -> p (a b)")[:, 0:16 * 128].rearrange("p (a b) -> p a b", b=128)
        while k < 256:
            ck = tabc[:, :, k - 1:k].to_broadcast([128, 16, k])
            sk = tabs[:, :, k - 1:k].to_broadcast([128, 16, k])
            c0 = tabc[:, :, 0:k]
            s0 = tabs[:, :, 0:k]
            a1 = t1[:, :, 0:k]
            a2 = t2[:, :, 0:k]
            P.tt(a1, c0, ck, ALU.mult)
            P.tt(a2, s0, sk, ALU.mult)
            P.tt(tabc[:, :, k:2 * k], a1, a2, ALU.subtract)
            P.tt(a1, s0, ck, ALU.mult)
            P.tt(a2, c0, sk, ALU.mult)
            P.tt(tabs[:, :, k:2 * k], a1, a2, ALU.add)
            k *= 2

    def wload(srcs):
        slot = wring[wslot[0] % NWR]
        wslot[0] += 1
        for ap, k0, n0 in srcs:
            kk, nn = ap.shape[1], ap.shape[2]
            P.dma("pool", slot[:, k0:k0 + kk, n0:n0 + nn], ap)
        return slot

    def wview(w2d):
        return w2d.rearrange("(k p) n -> p k n", p=128)

    def dbg_out(i, ap):
        if dbg_d is not None:
            P.dma("sp", dbg_d[:, i, :], ap)

    def layer_norm(l, s):
        p1 = nps()
        p2 = nps()
        sq = sc[:, 2:4, :]
        for c in range(NCH):
            P.act(sq[:, c % 2, :], x[:, c, :], AF.Square)
            P.mm(p1[:], cs_["ones_d"][:], x[:, c, :], start=(c == 0), stop=(c == NCH - 1))
            P.mm(p2[:], cs_["ones_d"][:], sq[:, c % 2, :], start=(c == 0), stop=(c == NCH - 1))
        mean = sc[:, 0, :]
        rstd = sc[:, 1, :]
        P.copy(mean, p1[:], eng="act")
        var = sc[:, 2, :]
        P.tt(var, mean, mean, ALU.mult)
        P.tt(var, p2[:], var, ALU.subtract)
        P.act(var, var, AF.Sqrt, bias=LN_EPS)
        P.recip(rstd, var)
        for c in range(NCH):
            t = sc[:, 2 + (c % 2), :]
            P.tt(t, x[:, c, :], mean, ALU.subtract)
            P.tt(t, t, rstd, ALU.mult)
            g = vcol("ln_g", l, s * 16 + c)
            b = vcol("ln_b", l, s * 16 + c)
            P.act(x[:, c, :], t, AF.Identity, bias=b, scale=g)
            P.copy(xb[:, c, :], x[:, c, :], eng="act" if c % 2 else "dve")

    def resid(c, psum_ap, scale):
        t = sc[:, 22 + (c % 2), :]
        P.act(t, psum_ap, AF.Copy, scale=scale)
        P.stt(x[:, c, :], x[:, c, :], ALU_ALPHA, t, ALU.mult, ALU.add)

    ALU_ALPHA = float(ALPHA)

    def ffn(l, f, s):
        h = sc[:, 0:22, :].bitcast(BF16).rearrange("p a (b c) -> p (a b) c", c=TB)
        wiv = wview(wi[l, f])
        sil = sc[:, 22:24, :].bitcast(BF16).rearrange("p a (b c) -> p (a b) c", c=TB)
        for jg in range(11):
            sg = wload([(wiv[:, :, jg * 512:(jg + 1) * 512], 0, 0)])
            pgs = [nps() for _ in range(4)]
            for jj in range(4):
                for k in range(NCH):
                    P.mm(pgs[jj][:], sg[:, k, jj * 128:(jj + 1) * 128], xb[:, k, :], start=(k == 0), stop=(k == NCH - 1))
            for jj in range(4):
                P.act(sil[:, jj, :], pgs[jj][:], AF.Silu)
            su = wload([(wiv[:, :, DFF + jg * 512:DFF + (jg + 1) * 512], 0, 0)])
            pus = [nps() for _ in range(4)]
            for jj in range(4):
                for k in range(NCH):
                    P.mm(pus[jj][:], su[:, k, jj * 128:(jj + 1) * 128], xb[:, k, :], start=(k == 0), stop=(k == NCH - 1))
            for jj in range(4):
                P.tt(h[:, jg * 4 + jj, :], sil[:, jj, :], pus[jj][:], ALU.mult)
        wov = wview(wo[l, f])
        for og in range(4):
            pss = [nps() for _ in range(4)]
            for kg, (k0, kn) in enumerate(((0, 16), (16, 16), (32, 12))):
                sl = wload([(wov[:, k0:k0 + kn, og * 512:(og + 1) * 512], 0, 0)])
                for oc in range(4):
                    for k in range(kn):
                        P.mm(pss[oc][:], sl[:, k, oc * 128:(oc + 1) * 128], h[:, k0 + k, :],
                             start=(k0 + k == 0), stop=(k0 + k == 43))
            for oc in range(4):
                resid(og * 4 + oc, pss[oc][:], 0.5)
        layer_norm(l, s)

    def ple(l, blk):
        pv = pT[l].rearrange("(k p) t -> p k t", p=128)
        P.dma("pool", pb[:], pv[:, :, blk * TB:(blk + 1) * TB])
        gv = wview(ple_gw[l])
        pwv = wview(ple_w[l])
        sgt = sc[:, 0:4, :]
        pwb = sc[:, 4:8, :].bitcast(BF16).rearrange("p a (b c) -> p (a b) c", c=512).rearrange("p (k g) c -> p k (g c)", k=2)
        P.dma("pool", pwb, pwv)
        for og in range(4):
            sg = wload([(gv[:, :, og * 512:(og + 1) * 512], 0, 0)])
            pgs = [nps() for _ in range(4)]
            for oc in range(4):
                for k in range(NCH):
                    P.mm(pgs[oc][:], sg[:, k, oc * 128:(oc + 1) * 128], xb[:, k, :], start=(k == 0), stop=(k == NCH - 1))
            for oc in range(4):
                P.act(sgt[:, oc, :], pgs[oc][:], AF.Sigmoid)
            pps = [nps() for _ in range(4)]
            for oc in range(4):
                c = og * 4 + oc
                for k in range(2):
                    P.mm(pps[oc][:], pwb[:, k, c * 128:(c + 1) * 128], pb[:, k, :], start=(k == 0), stop=(k == 1))
            for oc in range(4):
                c = og * 4 + oc
                P.tt(sgt[:, oc, :], sgt[:, oc, :], pps[oc][:], ALU.mult)
                P.stt(x[:, c, :], x[:, c, :], ALU_ALPHA, sgt[:, oc, :], ALU.mult, ALU.add)
        layer_norm(l, 3)

    def inproj(l, colstarts, widths, consume):
        wv = wview(w_in[l])
        i = 0
        while i < len(colstarts):
            grp = list(range(i, min(i + 4, len(colstarts))))
            srcs = []
            for gi, ci in enumerate(grp):
                srcs.append((wv[:, :, colstarts[ci]:colstarts[ci] + widths[ci]], 0, gi * 128))
            sl = wload(srcs)
            for gi, ci in enumerate(grp):
                p_ = nps()
                wd = widths[ci]
                for k in range(NCH):
                    P.mm(p_[0:wd, :], sl[:, k, gi * 128:gi * 128 + wd], xb[:, k, :], start=(k == 0), stop=(k == NCH - 1))
                consume(ci, p_)
            i += 4

    def conv4(l, name, chunk, tail_idx, src_ps, out_ap, cw, bias=None):
        P.copy(cw[:, 0:3], ctail[:, l, tail_idx, :], eng="act")
        P.copy(cw[:, 3:3 + TB], src_ps, eng="act")
        o = V(name, l) + chunk * 4
        kw = {}
        P.act(out_ap, cw[:, 0:TB], AF.Identity, scale=vec[:, o:o + 1], bias=bias if bias is not None else 0.0)
        for j in (1, 2, 3):
            P.stt(out_ap, cw[:, j:j + TB], vec[:, o + j:o + j + 1], out_ap, ALU.mult, ALU.add)
        P.copy(ctail[:, l, tail_idx, :], cw[:, TB:TB + 3], eng="act")

    def rms_feat(ones, src_list, eps, tmp_sq, out_rstd):
        p_ = nps()
        n = len(src_list)
        for i, s_ in enumerate(src_list):
            P.act(tmp_sq[:, i % 2, :], s_, AF.Square)
            P.mm(p_[:], ones[:], tmp_sq[:, i % 2, :], start=(i == 0), stop=(i == n - 1))
        P.act(out_rstd, p_[:], AF.Sqrt, bias=eps)
        P.recip(out_rstd, out_rstd)

    def gdn(l):
        dc = dcol[:, l, :]
        rows = sc[0:4, 20, :]
        RB = sc[0:4, 12, :]
        RG = sc[0:4, 13, :]
        REG = sc[0:4, 14, :]
        RBE = sc[0:4, 15, :]
        REL = sc[0:4, 16, :]

        def cons_bd(ci, p_):
            if ci == 0:
                P.act(RB, p_[0:4, :], AF.Sigmoid)
            else:
                dtb = vec[0:4, V("dtb", l):V("dtb", l) + 1]
                P.act(rows, p_[0:4, :], AF.Exp, bias=dtb)
                P.act(rows, rows, AF.Ln, bias=1.0)
                P.ts(rows, rows, dc[0:4, 80:81], None, ALU.mult)
                P.scan(RG, cs_["start64"][0:4, :], rows, 0.0)
                P.act(REG, RG, AF.Exp)
                P.tt(RBE, RB, REG, ALU.mult)
                g3 = RG.rearrange("p (a b) -> p a b", b=64)
                P.tt(REL.rearrange("p (a b) -> p a b", b=64), g3[:, :, 63:64].to_broadcast([4, 8, 64]), g3, ALU.subtract)
                P.act(REL, REL, AF.Exp)
        inproj(l, [2048, 2052], [4, 4], cons_bd)
        cols = sc[0:64, 17, 0:96].rearrange("p (c q h) -> p c q h", q=3, h=4)
        for c in range(8):
            for q, R in enumerate((RB, RBE, REL)):
                p_ = nps()
                P.tr(p_[0:64, 0:4], R[:, c * 64:(c + 1) * 64], ident[0:4, 0:4])
                P.copy(cols[:, c, q, :], p_[0:64, 0:4], eng="act")
        cwb = sc[:, 18:20, :].rearrange("p a b -> p (a b)")[:, 0:3 + TB]
        for h in range(4):
            qh = sc[:, 0, :]
            kh = sc[:, 1, :]
            vh = sc[:, 2, :]
            zh = sc[:, 3, :]
            kb = sc[:, 4, :]
            qg = sc[:, 5, :]
            tmp2 = sc[:, 6:8, :]
            rn = sc[:, 8, :]

            def cons(ci, p_, h=h):
                if ci < 3:
                    dst = (qh, kh, vh)[ci]
                    chunk = ci * 4 + h
                    conv4(l, "gconv", chunk, chunk, p_[:], dst, cwb)
                    P.act(dst, dst, AF.Silu)
                else:
                    P.act(zh, p_[:], AF.Silu)
            inproj(l, [h * 128, 512 + h * 128, 1024 + h * 128, 1536 + h * 128], [128] * 4, cons)
            rms_feat(cs_["ones_1"], [qh], RMS_EPS, tmp2, rn)
            P.stt(qh, qh, float(128 ** -0.5), rn, ALU.mult, ALU.mult)
            rms_feat(cs_["ones_1"], [kh], RMS_EPS, tmp2, rn)
            P.tt(kh, kh, rn, ALU.mult)
            pbe = nps()
            P.mm(pbe[:], cs_["sel"][:, h * 128:(h + 1) * 128], REG, start=True, stop=True)
            P.tt(qg, qh, pbe[:], ALU.mult)
            glc = sc[:, 9, 0:8]
            P.copy(glc, pbe[:].rearrange("p (c t) -> p c t", t=64)[:, :, 63], eng="act")
            pbb = nps()
            P.mm(pbb[:], cs_["sel"][:, h * 128:(h + 1) * 128], RB, start=True, stop=True)
            P.tt(kb, kh, pbb[:], ALU.mult)
            po = ps[7]
            S = gS[:, l, h, :]
            PA = sc[0:64, 6, :]
            TA = sc[0:64, 7, :]
            PB = sc[0:64, 10, :]
            TBb = sc[0:64, 11, :]
            QKT = sc[0:64, 8, :]
            Vm = sc[0:64, 18, :]
            selh = cs_["sel"][:, h * 128:h * 128 + 64]
            nselh = cs_["nsel"][:, h * 128:h * 128 + 64]

            def c3(a_):
                return a_.rearrange("p (c t) -> p c t", t=64)

            def cr(c):
                return slice(c * 64, (c + 1) * 64)
            pD = nps()
            for c in range(8):
                P.mm(pD[0:64, cr(c)], RG[:, cr(c)], selh, start=True, stop=False)
                P.mm(pD[0:64, cr(c)], nselh, RG[:, cr(c)], start=False, stop=False)
                P.mm(pD[0:64, cr(c)], ident[0:64, 0:64], cs_["m_strict_add"][:], start=False, stop=True)
            P.act(PB, pD[0:64, :], AF.Exp)
            pDT = nps()
            for c in range(8):
                P.mm(pDT[0:64, cr(c)], selh, RG[:, cr(c)], start=True, stop=False)
                P.mm(pDT[0:64, cr(c)], RG[:, cr(c)], nselh, start=False, stop=False)
                P.mm(pDT[0:64, cr(c)], ident[0:64, 0:64], cs_["m_inclT_add"][:], start=False, stop=True)
            P.act(TBb, pDT[0:64, :], AF.Exp)
            pA = nps()
            for c in range(8):
                P.mm(pA[0:64, cr(c)], kb[:, cr(c)], kh[:, cr(c)])
            P.tt(PA, pA[0:64, :], PB, ALU.mult)
            pAT = nps()
            for c in range(8):
                P.mm(pAT[0:64, cr(c)], kh[:, cr(c)], kb[:, cr(c)])
            P.tt(TA, pAT[0:64, :], TBb, ALU.mult)
            P.tt(c3(TA), c3(TA), cs_["m_strictT_01"][:].unsqueeze(1).to_broadcast([64, 8, 64]), ALU.mult)
            pQK = nps()
            for c in range(8):
                P.mm(pQK[0:64, cr(c)], kh[:, cr(c)], qh[:, cr(c)])
            P.tt(QKT, pQK[0:64, :], TBb, ALU.mult)
            P.tt(c3(Vm), ident[0:64, 0:64].unsqueeze(1).to_broadcast([64, 8, 64]), c3(TA), ALU.subtract)
            Pc, Tc, Pn2, Tn2 = PA, TA, PB, TBb
            for j in range(5):
                pq = nps()
                for c in range(8):
                    P.mm(pq[0:64, cr(c)], Tc[:, cr(c)], Pc[:, cr(c)])
                if j < 4:
                    pq2 = nps()
                    for c in range(8):
                        P.mm(pq2[0:64, cr(c)], Pc[:, cr(c)], Tc[:, cr(c)])
                P.copy(Pn2, pq[0:64, :], eng="act")
                if j < 4:
                    P.copy(Tn2, pq2[0:64, :])
                pv_ = nps()
                for c in range(8):
                    P.mm(pv_[0:64, cr(c)], Pn2[:, cr(c)], Vm[:, cr(c)])
                P.tt(Vm, Vm, pv_[0:64, :], ALU.add)
                Pc, Pn2 = Pn2, Pc
                Tc, Tn2 = Tn2, Tc

            def v8(lo):
                return sc[0:64, lo:lo + 2, :].rearrange("p a (c d) -> p (a c) d", d=128)
            rhs_w = v8(6)
            rhs_u = v8(10)
            kd = v8(19)
            uu = v8(21)
            wT = sc[:, 23, :]
            for g in range(2):
                pt = nps()
                for cc in range(4):
                    P.tr(pt[0:64, cc * 128:(cc + 1) * 128], kh[:, cr(4 * g + cc)], ident[:])
                ptv = pt[0:64, :].rearrange("p (c d) -> p c d", d=128)
                P.tt(rhs_w[:, 4 * g:4 * g + 4, :], ptv, cols[:, 4 * g:4 * g + 4, 1, h].unsqueeze(2).to_broadcast([64, 4, 128]), ALU.mult)
                P.tt(kd[:, 4 * g:4 * g + 4, :], ptv, cols[:, 4 * g:4 * g + 4, 2, h].unsqueeze(2).to_broadcast([64, 4, 128]), ALU.mult)
            for g in range(2):
                pt = nps()
                for cc in range(4):
                    P.tr(pt[0:64, cc * 128:(cc + 1) * 128], vh[:, cr(4 * g + cc)], ident[:])
                ptv = pt[0:64, :].rearrange("p (c d) -> p c d", d=128)
                P.tt(rhs_u[:, 4 * g:4 * g + 4, :], ptv, cols[:, 4 * g:4 * g + 4, 0, h].unsqueeze(2).to_broadcast([64, 4, 128]), ALU.mult)
            pw = nps()
            for c in range(8):
                P.mm(pw[:, cr(c)], rhs_w[:, c, :], Vm[:, cr(c)])
            P.copy(wT, pw[:], eng="act")
            for g in range(2):
                pu = nps()
                for cc in range(4):
                    c = 4 * g + cc
                    P.mm(pu[0:64, cc * 128:(cc + 1) * 128], Vm[:, cr(c)], rhs_u[:, c, :])
                P.copy(uu[:, 4 * g:4 * g + 4, :], pu[0:64, :].rearrange("p (c d) -> p c d", d=128), eng="act")
            for c in range(8):
                pws = nps()
                P.mm(pws[0:64, 0:128], wT[:, cr(c)], S)
                P.tt(uu[:, c, :], uu[:, c, :], pws[0:64, 0:128], ALU.subtract)
                P.mm(po[:, cr(c)], S, qg[:, cr(c)], start=True, stop=False)
                P.mm(po[:, cr(c)], uu[:, c, :], QKT[:, cr(c)], start=False, stop=True)
                pds = nps()
                P.mm(pds[:, 0:128], kd[:, c, :], uu[:, c, :])
                P.stt(S, S, glc[:, c:c + 1], pds[:, 0:128], ALU.mult, ALU.add)
            o = sc[:, 4, :]
            P.copy(o, po[:], eng="act")
            rms_feat(cs_["ones_h"], [o], RMS_EPS, tmp2, rn)
            P.tt(o, o, rn, ALU.mult)
            P.stt(ybuf[:, h, :], o, vcol("gnorm", l, 0), zh, ALU.mult, ALU.mult)

    def hgrn(l):
        dc = dcol[:, l, :]
        for h in range(4):
            q = sc[:, 0, :]
            fl = sc[:, 1, :]
            iv = sc[:, 2, :]
            gz = sc[:, 3, :]
            kk = sc[:, 4, :]
            b = sc[:, 5, :]
            qt = sc[:, 6, :]
            kt = sc[:, 7, :]
            qb = sc[:, 8, :]
            khh = sc[:, 9, :]
            tmp = sc[:, 10, :]
            ebl = sc[:, 11, 0:16]
            tmp2 = sc[:, 12:14, :]
            rn = sc[:, 14, :]

            def cons(ci, p_):
                if ci == 0:
                    P.copy(q, p_[:], eng="act")
                elif ci == 1:
                    P.act(fl, p_[:], AF.Sigmoid)
                elif ci == 2:
                    P.copy(iv, p_[:], eng="act")
                else:
                    P.act(gz, p_[:], AF.Silu)
            base = 2056
            inproj(l, [base + h * 128, base + 512 + h * 128, base + 1024 + h * 128, base + 1536 + h * 128], [128] * 4, cons)
            P.ts(fl, fl, dc[:, 76 + h:77 + h], dc[:, 72 + h:73 + h], ALU.mult, ALU.add)
            P.ts(kk, fl, -1.0, 1.0, ALU.mult, ALU.add)
            P.act(fl, fl, AF.Ln)
            P.scan(b, cs_["start32"][:], fl, 0.0)
            b3 = b.rearrange("p (a c) -> p a c", c=32)
            t3 = tmp.rearrange("p (a c) -> p a c", c=32)
            P.tt(t3, b3, b3[:, :, 15:16].to_broadcast([128, 16, 32]), ALU.subtract)
            P.act(qt, tmp, AF.Exp)
            P.act(kt, tmp, AF.Exp, scale=-1.0)
            P.stt(qt, q, float(128 ** -0.5), qt, ALU.mult, ALU.mult)
            P.tt(kt, kk, kt, ALU.mult)
            P.act(qb, b, AF.Exp)
            P.stt(qb, q, float(128 ** -0.5), qb, ALU.mult, ALU.mult)
            P.tt(t3, b3[:, :, 31:32].to_broadcast([128, 16, 32]), b3, ALU.subtract)
            P.act(khh, tmp, AF.Exp)
            P.tt(khh, kk, khh, ALU.mult)
            P.act(ebl, b3[:, :, 31], AF.Exp)
            po = ps[7]
            S = hS[:, l, h, :]
            attT_all = sc[0:32, 15, :]
            vT_all = sc[0:32, 16:20, :]
            khT_all = sc[0:32, 20:24, :]
            pa = nps()
            for c in range(16):
                cs = slice(c * 32, (c + 1) * 32)
                P.mm(pa[0:32, cs], kt[:, cs], qt[:, cs])
            P.stt(attT_all.rearrange("p (c t) -> p c t", t=32), pa[0:32, :].rearrange("p (c t) -> p c t", t=32), 1e30,
                  cs_["m_inclT_01"][0:32, 0:32].unsqueeze(1).to_broadcast([32, 16, 32]), ALU.min, ALU.mult)
            for g in range(4):
                pt = nps()
                for cc in range(4):
                    c = 4 * g + cc
                    P.tr(pt[0:32, cc * 128:(cc + 1) * 128], iv[:, c * 32:(c + 1) * 32], ident[:])
                P.copy(vT_all[:, g, :], pt[0:32, :], eng="act")
            for g in range(4):
                pt = nps()
                for cc in range(4):
                    c = 4 * g + cc
                    P.tr(pt[0:32, cc * 128:(cc + 1) * 128], khh[:, c * 32:(c + 1) * 32], ident[:])
                P.copy(khT_all[:, g, :], pt[0:32, :])
            for c in range(16):
                cs = slice(c * 32, (c + 1) * 32)
                attT = attT_all[:, cs]
                vT = vT_all[:, c // 4, (c % 4) * 128:(c % 4 + 1) * 128]
                khT = khT_all[:, c // 4, (c % 4) * 128:(c % 4 + 1) * 128]
                P.mm(po[:, cs], S, qb[:, cs], start=True, stop=False)
                P.mm(po[:, cs], vT, attT, start=False, stop=True)
                pds = nps()
                P.mm(pds[:, 0:128], khT, vT)
                P.stt(S, S, ebl[:, c:c + 1], pds[:, 0:128], ALU.mult, ALU.add)
            o = sc[:, 4, :]
            P.copy(o, po[:], eng="act")
            rms_feat(cs_["ones_h"], [o], RMS_EPS, tmp2, rn)
            P.tt(o, o, rn, ALU.mult)
            P.stt(ybuf[:, 4 + h, :], o, vcol("hnorm", l, 0), gz, ALU.mult, ALU.mult)

    def s5(l):
        dc = dcol[:, l, :]
        build_tables(l)
        u = sc[:, 0:4, :]

        def cons(ci, p_):
            P.copy(u[:, ci, :], p_[:], eng="act")
        inproj(l, [4104 + i * 128 for i in range(4)], [128] * 4, cons)
        gy = sc[:, 4:8, :]
        Bs = sc[:, 8:10, :].rearrange("p a b -> p (a b)").rearrange("p (r j m) -> p r j m", r=2, j=4)
        Cs = sc[:, 10:12, :].rearrange("p a b -> p (a b)").rearrange("p (r j m) -> p r j m", r=2, j=4)
        for i in range(4):
            P.dma("sp", Bs, Bp_d[l][:, :, 4 * i:4 * i + 4, :])
            P.dma("sp", Cs, Cp_d[l][:, :, 4 * i:4 * i + 4, :])
            py = ps[6]
            for jj in range(4):
                j = 4 * i + jj
                pr = nps()
                pi = nps()
                P.mm(pr[:], Bs[:, 0, jj, :], u[:, i, :])
                P.mm(pi[:], Bs[:, 1, jj, :], u[:, i, :])
                bre = sc[:, 12, :]
                bim = sc[:, 13, :]
                t1 = sc[:, 14, :]
                t2 = sc[:, 15, :]
                P.act(t1, pr[:], AF.Identity, scale=dc[:, 16 + j:17 + j])
                P.stt(bre, pi[:], dc[:, 48 + j:49 + j], t1, ALU.mult, ALU.add)
                P.act(t2, pi[:], AF.Identity, scale=dc[:, 16 + j:17 + j])
                P.stt(bim, pr[:], dc[:, 32 + j:33 + j], t2, ALU.mult, ALU.add)
                cb = tabc[:, j, :].unsqueeze(1).to_broadcast([128, 2, 256])
                sbb = tabs[:, j, :].unsqueeze(1).to_broadcast([128, 2, 256])

                def v3(a):
                    return a.rearrange("p (a b) -> p a b", b=256)
                wr = sc[:, 16, :]
                wi_ = sc[:, 17, :]
                sr = sc[:, 22, :]
                si = sc[:, 23, :]
                P.tt(v3(t1), v3(bre), cb, ALU.mult)
                P.tt(v3(t2), v3(bim), sbb, ALU.mult)
                P.tt(wr, t1, t2, ALU.add)
                P.tt(v3(t1), v3(bim), cb, ALU.mult)
                P.tt(v3(t2), v3(bre), sbb, ALU.mult)
                P.tt(wi_, t1, t2, ALU.subtract)
                xre = sc[:, 18 + 2 * (jj % 2), :]
                nxim = sc[:, 19 + 2 * (jj % 2), :]
                rb = dc[:, j:j + 1].to_broadcast([128, 256])
                for hf in range(2):
                    hs = slice(hf * 256, (hf + 1) * 256)
                    P.scan(sr[:, hs], rb, wr[:, hs], s5st[:, l, j, 0:1])
                    P.scan(si[:, hs], rb, wi_[:, hs], s5st[:, l, j, 1:2])
                    P.tt(t1[:, hs], sr[:, hs], tabc[:, j, :], ALU.mult)
                    P.tt(t2[:, hs], si[:, hs], tabs[:, j, :], ALU.mult)
                    P.tt(xre[:, hs], t1[:, hs], t2[:, hs], ALU.subtract)
                    P.tt(t1[:, hs], sr[:, hs], tabs[:, j, :], ALU.mult)
                    P.tt(t2[:, hs], si[:, hs], tabc[:, j, :], ALU.mult)
                    P.stt(nxim[:, hs], t1[:, hs], -1.0, t2[:, hs], ALU.mult, ALU.subtract)
                    P.copy(s5st[:, l, j, 0:1], xre[:, hf * 256 + 255:hf * 256 + 256], eng="act")
                    P.act(s5st[:, l, j, 1:2], nxim[:, hf * 256 + 255:hf * 256 + 256], AF.Copy, scale=-1.0)
                P.mm(py[:], Cs[:, 0, jj, :], xre, start=(jj == 0), stop=False)
                P.mm(py[:], Cs[:, 1, jj, :], nxim, start=False, stop=(jj == 3))
            yv = sc[:, 12, :]
            P.stt(yv, u[:, i, :], vcol("s5D", l, i), py[:], ALU.mult, ALU.add)
            P.act(gy[:, i, :], yv, AF.Gelu_apprx_tanh)
        gl = sc[:, 8:12, :]
        P.dma("sp", gl, glu_d[l])
        so = sc[:, 12:16, :]
        for oc in range(4):
            pg = nps()
            for k in range(4):
                P.mm(pg[:], gl[:, k, oc * 128:(oc + 1) * 128], gy[:, k, :], start=(k == 0), stop=(k == 3))
            t = sc[:, 16, :]
            P.act(t, pg[:], AF.Sigmoid, bias=vcol("glub", l, oc))
            P.tt(so[:, oc, :], gy[:, oc, :], t, ALU.mult)
        rn = sc[:, 17, :]
        rms_feat(cs_["ones_w"], [so[:, i, :] for i in range(4)], RMS_EPS, sc[:, 18:20, :], rn)
        for i in range(4):
            P.stt(ybuf[:, 8 + i, :], so[:, i, :], vcol("bng", l, i), rn, ALU.mult, ALU.mult)

    def lru(l):
        dc = dcol[:, l, :]
        Wb = sc[:, 8:10, :].rearrange("p a b -> p (a b)").rearrange("p (r j m) -> p r j m", r=2, j=4)
        P.dma("sp", Wb, Wb_d[l])
        xc = sc[:, 0:4, :]
        gg = sc[:, 4:8, :]
        cwb = sc[:, 10:12, :].rearrange("p a b -> p (a b)")[:, 0:3 + TB]

        def cons(ci, p_):
            if ci < 4:
                conv4(l, "lconv", ci, 12 + ci, p_[:], xc[:, ci, :], cwb, bias=vcol("lconvb", l, ci))
            else:
                P.act(gg[:, ci - 4, :], p_[:], AF.Gelu_apprx_tanh)
        inproj(l, [4616 + i * 128 for i in range(8)], [128] * 8, cons)
        yo = sc[:, 12:16, :]
        for i in range(4):
            pr = nps()
            pi = nps()
            P.mm(pr[:], Wb[:, 0, i, :], xc[:, i, :])
            P.mm(pi[:], Wb[:, 1, i, :], xc[:, i, :])
            r = sc[:, 16, :]
            gi = sc[:, 17, :]
            a = sc[:, 18, :]
            a2 = sc[:, 19, :]
            P.act(r, pr[:], AF.Sigmoid, bias=vcol("lba", l, i))
            P.act(gi, pi[:], AF.Sigmoid, bias=vcol("lbx", l, i))
            P.act(a, r, AF.Exp, scale=dc[:, 64 + i:65 + i])
            P.act(a2, r, AF.Exp, scale=dc[:, 68 + i:69 + i])
            P.ts(a2, a2, -1.0, 1.0, ALU.mult, ALU.add)
            P.act(a2, a2, AF.Sqrt)
            P.tt(gi, gi, xc[:, i, :], ALU.mult)
            P.tt(gi, gi, a2, ALU.mult)
            hh = sc[:, 20, :]
            P.scan(hh, a, gi, lrust[:, l, i:i + 1])
            P.copy(lrust[:, l, i:i + 1], hh[:, TB - 1:TB], eng="act")
            P.tt(yo[:, i, :], hh, gg[:, i, :], ALU.mult)
        rn = sc[:, 16, :]
        rms_feat(cs_["ones_w"], [yo[:, i, :] for i in range(4)], RMS_EPS, sc[:, 18:20, :], rn)
        for i in range(4):
            P.stt(ybuf[:, 12 + i, :], yo[:, i, :], vcol("bng", l, 4 + i), rn, ALU.mult, ALU.mult)

    def mixer(l):
        if 'gdn' in ST:
            gdn(l)
        if 'hgrn' in ST:
            hgrn(l)
        if 's5' in ST:
            s5(l)
        if 'lru' in ST:
            lru(l)
        if 'oproj' not in ST:
            return
        wv = wview(w_out[l])
        for og in range(4):
            sl = wload([(wv[:, :, og * 512:(og + 1) * 512], 0, 0)])
            for oc in range(4):
                p_ = nps()
                for k in range(NCH):
                    P.mm(p_[:], sl[:, k, oc * 128:(oc + 1) * 128], ybuf[:, k, :], start=(k == 0), stop=(k == NCH - 1))
                resid(og * 4 + oc, p_[:], 1.0)
        layer_norm(l, 1)

    xv = xT.rearrange("(c p) t -> p c t", p=128)
    ov = oT.rearrange("(c p) t -> p c t", p=128)
    if pipe:
        GROUPS = [[b_, b_ + int(pipe)] for b_ in range(int(pipe))]
    for blk in range(NST):
        P.dma("sp", x[:], xv[:, :, blk * TB:(blk + 1) * TB])
        if pipe:
            if blk > 0:
                xr = sc[:, 0:16, :]
                for j in range(4):
                    P.dma("sp", xr[:, 4 * j:4 * j + 4, :], cc_recv[j].ap()[0:512, :].rearrange("(c p) t -> p c t", p=128))
            for c in range(NCH if blk > 0 else 0):
                P.ts(x[:, c, :], x[:, c, :], cmask[:, 0:1], None, ALU.mult)
                P.stt(x[:, c, :], xr[:, c, :], cmask[:, 1:2], x[:, c, :], ALU.mult, ALU.add)
        for c in range(NCH):
            P.copy(xb[:, c, :], x[:, c, :], eng="act" if c % 2 else "dve")
        for l in range(depth):
            if 'ffn1' in ST:
                ffn(l, 0, 0)
            mixer(l)
            if 'ffn2' in ST:
                ffn(l, 1, 2)
            if 'ple' in ST:
                ple(l, blk)
        P.dma("sp", ov[:, :, blk * TB:(blk + 1) * TB], x[:])
        if pipe and blk < NST - 1:
            for j in range(4):
                P.dma("sp", cc_send[j].ap().rearrange("(c p) t -> p c t", p=128), x[:, 4 * j:4 * j + 4, :])
                P.cc(GROUPS, cc_send[j].ap(), cc_recv[j].ap())
        if pipe and blk == 0:
            for t_ in (gS, hS, s5st, lrust, ctail):
                P.ts(t_[:], t_[:], cmask[:, 0:1], None, ALU.mult)
    P.wait_all_dma("sp")
    P.emit(st)
    st.close()
    return nc, P


_CACHE = {}


def prep_inputs(inp, depth, extra_vec=None):
    vp = build_vec(inp, depth)
    for k_, v_ in (extra_vec or {}).items():
        vp.add(k_, v_)
    vecarr = vp.build()
    mats = [build_mats(inp, l) for l in range(depth)]
    shared = {
        "ffn_wi": np.ascontiguousarray(inp["ffn_wi"][:depth], dtype=np.float32),
        "ffn_wo": np.ascontiguousarray(inp["ffn_wo"][:depth], dtype=np.float32),
        "mix_w_in": np.ascontiguousarray(inp["mix_w_in"][:depth], dtype=np.float32),
        "mix_w_out": np.ascontiguousarray(inp["mix_w_out"][:depth], dtype=np.float32),
        "ple_w": np.ascontiguousarray(inp["ple_w"][:depth], dtype=np.float32),
        "ple_gate_w": np.ascontiguousarray(inp["ple_gate_w"][:depth], dtype=np.float32),
        "vec": vecarr,
        "Bp": np.stack([m[0] for m in mats]),
        "Cp": np.stack([m[1] for m in mats]),
        "glu": np.stack([m[2] for m in mats]),
        "Wb": np.stack([m[3] for m in mats]),
    }
    for k, v in build_consts().items():
        shared["c_" + k] = v
    return shared, vp.index, vecarr.shape[1]


def run_model(inp, depth=2, n_cores=8, dbg=None, stages=None):
    inp = {k: np.asarray(v) for k, v in inp.items()}
    x = inp["x"]
    p = inp["p"]
    B, T, _ = x.shape
    shared, vidx, nvec = prep_inputs(inp, depth)
    key = (T, depth, nvec, dbg, None if stages is None else tuple(sorted(stages)))
    if key not in _CACHE:
        _CACHE[key] = build_program(T, depth, vidx, nvec, dbg, stages)
    nc, P = _CACHE[key]
    in_maps = []
    for c in range(n_cores):
        b = c % B
        m = dict(shared)
        m["xT"] = np.ascontiguousarray(x[b].T)
        m["pT"] = np.ascontiguousarray(p[:depth, b].transpose(0, 2, 1))
        m["cmask"] = np.ones((128, 4), np.float32)
        in_maps.append(m)
    res = run_bass_kernel_spmd(nc, in_maps, core_ids=list(range(n_cores)))
    out = np.stack([np.ascontiguousarray(res.results[b]["oT"].T) for b in range(B)])
    if dbg is not None:
        return out, res.results[0]["dbg"]
    return out


LAYER_KEYS = ("ln_g", "ln_b", "ffn_wi", "ffn_wo", "mix_w_in", "mix_w_out", "gdn_conv_w", "gdn_A_log", "gdn_dt_bias",
              "gdn_norm_g", "hgrn_lb_logits", "hgrn_norm_g", "s5_lam_re", "s5_lam_im", "s5_log_dt", "s5_B_re", "s5_B_im",
              "s5_C_re", "s5_C_im", "s5_D", "s5_glu_w", "s5_glu_b", "lru_conv_w", "lru_conv_b", "lru_wa", "lru_ba",
              "lru_wx", "lru_bx", "lru_param", "branch_norm_g", "ple_w", "ple_gate_w")


def pipe_maps(inp, n_seq, sim=False):
    x = inp["x"]
    p = inp["p"]
    B, T, _ = x.shape
    NB = T // TB
    TT = (NB + 1) * TB
    per_layer = []
    vidx = nvec = None
    for l in range(2):
        il = {k: inp[k][l:l + 1] for k in LAYER_KEYS}
        shared, vidx_l, nvec_l = prep_inputs(il, 1, extra_vec={"hlbA": chunkcols(inp["hgrn_lb_logits"][0]),
                                                                "hlbB": chunkcols(inp["hgrn_lb_logits"][1])})
        per_layer.append(shared)
        vidx, nvec = vidx_l, nvec_l
    maps = {}
    for l in range(2):
        for b in range(n_seq):
            m = dict(per_layer[l])
            xt = np.zeros((D, TT), np.float32)
            pt = np.zeros((1, 256, TT), np.float32)
            cm = np.zeros((128, 4), np.float32)
            if l == 0:
                xt[:, :T] = x[b].T
                xt[:, T:] = x[b, T - TB:].T
                pt[0, :, :T] = p[0, b].T
                cm[:, 0] = 1.0
            else:
                xt[:, :TB] = x[b, :TB].T
                pt[0, :, TB:] = p[1, b].T
                cm[:, 1] = 1.0
            m["xT"] = xt
            m["pT"] = pt
            m["cmask"] = cm
            maps[(l, b)] = m
    return maps, vidx, nvec


def run_model_pipe(inp):
    inp = {k: np.asarray(v) for k, v in inp.items()}
    B, T, _ = inp["x"].shape
    maps, vidx, nvec = pipe_maps(inp, B)
    key = ("pipe", T, nvec)
    if key not in _CACHE:
        _CACHE[key] = build_program(T, 1, vidx, nvec, None, None, pipe=4)
    nc, P = _CACHE[key]
    in_maps = [maps[(c // 4, c % 4)] for c in range(8)]
    res = run_bass_kernel_spmd(nc, in_maps, core_ids=list(range(8)))
    return np.stack([np.ascontiguousarray(res.results[4 + b]["oT"][:, TB:].T) for b in range(B)])


def kernel(**inputs):
    return run_model_pipe(inputs).astype(np.float32)
```
